# Optimizing a Trainium2 kernel written in Bass

```python
import math
import jax, jax.numpy as jnp
from jax import lax
import numpy as np

D_MODEL = 2048
BATCH = 4
SEQ = 2048
DEPTH = 4
DEC_BATCH = 32
DEC_SEQ = 1
PAST_LEN = 16384
PAGE_SIZE = 128

D_MIX = D_MODEL
D_POOL = D_MIX // 4
POOL_WINDOWS = (2, 4, 8, 16)
N_POOL_GROUPS = len(POOL_WINDOWS)
POOL_GROUP = D_POOL // N_POOL_GROUPS
POOL_STATE = max(POOL_WINDOWS) - 1
D_CONV = D_MIX // 4
CONV_WIDTH = 31
CONV_STATE = CONV_WIDTH - 1
D_ATTN = D_MIX // 2
HEAD_DIM = 64
N_HEADS = D_ATTN // HEAD_DIM
N_KV_HEADS = 4
GQA_GROUP = N_HEADS // N_KV_HEADS
D_KV = N_KV_HEADS * HEAD_DIM
WINDOW = 128
BLOCK = WINDOW
ATTN_SCALE = HEAD_DIM ** -0.5
NEG_INF = -1e30
N_BUCKETS = 32
MAX_EXACT = N_BUCKETS // 2
MAX_DISTANCE = 128
D_IN = D_POOL + 2 * D_CONV + D_ATTN + 2 * D_KV
SPLITS = (D_POOL, D_POOL + D_CONV, D_POOL + 2 * D_CONV,
          D_POOL + 2 * D_CONV + D_ATTN, D_POOL + 2 * D_CONV + D_ATTN + D_KV)
D_FF = 5632
D_PLE = 256
LN_EPS = 1e-5
ALPHA = (2 * DEPTH) ** 0.25
BETA = (8 * DEPTH) ** -0.25

kernel_name = 'hybrid_pool_conv_swa_macaron_deepnorm_step'


def _layer_norm(x, g, b):
    xf = x.astype(jnp.float32)
    mu = jnp.mean(xf, axis=-1, keepdims=True)
    var = jnp.mean(jnp.square(xf - mu), axis=-1, keepdims=True)
    y = (xf - mu) * lax.rsqrt(var + LN_EPS) * g.astype(jnp.float32) + b.astype(jnp.float32)
    return y.astype(x.dtype)


def _swiglu(x, w_gate, w_up, w_down):
    return (jax.nn.silu(x @ w_gate) * (x @ w_up)) @ w_down


def _t5_bucket(dist):
    n = jnp.maximum(dist, 0)
    nf = jnp.maximum(n, 1).astype(jnp.float32)
    log_b = MAX_EXACT + (jnp.log(nf / MAX_EXACT) / math.log(MAX_DISTANCE / MAX_EXACT)
                         * (N_BUCKETS - MAX_EXACT)).astype(jnp.int32)
    return jnp.where(n < MAX_EXACT, n, jnp.minimum(log_b, N_BUCKETS - 1))


def _multiscale_pool(u_ext, positions, w_pool, pool_scale):
    L = positions.shape[0]
    uf = u_ext.astype(jnp.float32)
    c = jnp.pad(jnp.cumsum(uf, axis=1), ((0, 0), (1, 0), (0, 0)))
    cur = uf[:, POOL_STATE:]
    diffs = []
    for g, w in enumerate(POOL_WINDOWS):
        sl = slice(g * POOL_GROUP, (g + 1) * POOL_GROUP)
        win_sum = (c[:, POOL_STATE + 1:POOL_STATE + 1 + L, sl]
                   - c[:, POOL_STATE + 1 - w:POOL_STATE + 1 - w + L, sl])
        count = jnp.minimum(positions + 1, w).astype(jnp.float32)[None, :, None]
        diffs.append(win_sum / count - cur[..., sl])
    d = jnp.stack(diffs, axis=2).astype(u_ext.dtype)
    y = jnp.einsum('blgc,gcd->blgd', d, w_pool)
    return y.reshape(y.shape[0], L, D_POOL) * pool_scale


def _conformer_conv(v_ext, w_dw, b_dw, ln_g, ln_b, w_pw):
    y = lax.conv_general_dilated(v_ext, w_dw[:, None, :], window_strides=(1,), padding='VALID',
                                 dimension_numbers=('NWC', 'WIO', 'NWC'),
                                 feature_group_count=D_CONV)
    y = jax.nn.silu(_layer_norm(y + b_dw, ln_g, ln_b))
    return y @ w_pw


def _window_softmax(q, k, v, q_pos, k_pos, rel_bias, sinks):
    B, NB, Q = q.shape[:3]
    K = k.shape[2]
    qg = q.reshape(B, NB, Q, N_KV_HEADS, GQA_GROUP, HEAD_DIM)
    s = jnp.einsum('bnqhgd,bnshd->bnhgqs', qg, k,
                   preferred_element_type=jnp.float32) * ATTN_SCALE
    dist = q_pos[:, :, None] - k_pos[:, None, :]
    valid = (dist >= 0) & (dist < WINDOW) & (k_pos[:, None, :] >= 0)
    bias = rel_bias[_t5_bucket(dist)].astype(jnp.float32)
    bias = jnp.moveaxis(bias, -1, 1).reshape(NB, N_KV_HEADS, GQA_GROUP, Q, K)
    s = jnp.where(valid[None, :, None, None], s + bias[None], NEG_INF)
    sink = sinks.astype(jnp.float32).reshape(1, 1, N_KV_HEADS, GQA_GROUP, 1, 1)
    m = jnp.maximum(jnp.max(s, axis=-1, keepdims=True), sink)
    e = jnp.exp(s - m)
    prob = e / (jnp.sum(e, axis=-1, keepdims=True) + jnp.exp(sink - m))
    o = jnp.einsum('bnhgqs,bnshd->bnqhgd', prob.astype(v.dtype), v)
    return o.reshape(B, NB, Q, D_ATTN)


def _token_mixer(h, pos0, pool_past, conv_past, k_past, v_past, prm, l):
    B, L, _ = h.shape
    z = h @ prm['w_in'][l]
    u, a, gt, q, k, v = jnp.split(z, SPLITS, axis=-1)
    positions = pos0 + jnp.arange(L, dtype=jnp.int32)
    u_ext = jnp.concatenate([pool_past, u], axis=1)
    y_pool = _multiscale_pool(u_ext, positions, prm['w_pool'][l], prm['pool_scale'][l])
    c_ext = jnp.concatenate([conv_past, a * jax.nn.sigmoid(gt)], axis=1)
    y_conv = _conformer_conv(c_ext, prm['w_dw'][l], prm['b_dw'][l], prm['conv_ln_g'][l],
                             prm['conv_ln_b'][l], prm['w_pw'][l])
    q = q.reshape(B, L, N_HEADS, HEAD_DIM)
    k = k.reshape(B, L, N_KV_HEADS, HEAD_DIM)
    v = v.reshape(B, L, N_KV_HEADS, HEAD_DIM)
    if k_past is None:
        nb = L // BLOCK
        qb = q.reshape(B, nb, BLOCK, N_HEADS, HEAD_DIM)
        kb = k.reshape(B, nb, BLOCK, N_KV_HEADS, HEAD_DIM)
        vb = v.reshape(B, nb, BLOCK, N_KV_HEADS, HEAD_DIM)
        prev = lambda t: jnp.pad(t, ((0, 0), (1, 0), (0, 0), (0, 0), (0, 0)))[:, :-1]
        k_band = jnp.concatenate([prev(kb), kb], axis=2)
        v_band = jnp.concatenate([prev(vb), vb], axis=2)
        q_pos = positions.reshape(nb, BLOCK)
        k_pos = (pos0 + (jnp.arange(nb, dtype=jnp.int32)[:, None] - 1) * BLOCK
                 + jnp.arange(2 * BLOCK, dtype=jnp.int32)[None])
        y_attn = _window_softmax(qb, k_band, v_band, q_pos, k_pos, prm['rel_bias'],
                                 prm['sinks'][l]).reshape(B, L, D_ATTN)
        k_keep, v_keep = k[:, -WINDOW:], v[:, -WINDOW:]
    else:
        k_ext = jnp.concatenate([k_past, k], axis=1)
        v_ext = jnp.concatenate([v_past, v], axis=1)
        q_pos = positions[None]
        k_pos = (pos0 - WINDOW + jnp.arange(WINDOW + L, dtype=jnp.int32))[None]
        y_attn = _window_softmax(q[:, None], k_ext[:, None], v_ext[:, None], q_pos, k_pos,
                                 prm['rel_bias'], prm['sinks'][l]).reshape(B, L, D_ATTN)
        k_keep, v_keep = k_ext[:, -WINDOW:], v_ext[:, -WINDOW:]
    y = jnp.concatenate([y_pool, y_conv, y_attn], axis=-1) @ prm['w_out'][l]
    return y, (k_keep, v_keep, u_ext[:, -POOL_STATE:], c_ext[:, -CONV_STATE:])


def _trunk(x, p, pos0, cache_k, cache_v, state_pool, state_conv, prm):
    prompt = cache_k is None
    B = x.shape[0]
    h = x
    new = ([], [], [], [])
    for l in range(DEPTH):
        h = _layer_norm(ALPHA * h + 0.5 * _swiglu(h, prm['ffn1_w_gate'][l], prm['ffn1_w_up'][l],
                                                  prm['ffn1_w_down'][l]),
                        prm['ln1_g'][l], prm['ln1_b'][l])
        if prompt:
            pool_past = jnp.zeros((B, POOL_STATE, D_POOL), h.dtype)
            conv_past = jnp.zeros((B, CONV_STATE, D_CONV), h.dtype)
            k_past, v_past = None, None
        else:
            pool_past, conv_past = state_pool[l], state_conv[l]
            k_past, v_past = cache_k[l], cache_v[l]
        y, st = _token_mixer(h, pos0, pool_past, conv_past, k_past, v_past, prm, l)
        h = _layer_norm(ALPHA * h + y, prm['ln2_g'][l], prm['ln2_b'][l])
        h = _layer_norm(ALPHA * h + 0.5 * _swiglu(h, prm['ffn2_w_gate'][l], prm['ffn2_w_up'][l],
                                                  prm['ffn2_w_down'][l]),
                        prm['ln3_g'][l], prm['ln3_b'][l])
        h = h + jax.nn.sigmoid(h @ prm['w_ple_gate'][l]) * (p[l] @ prm['w_ple'][l])
        for lst, s in zip(new, st):
            lst.append(s)
    return h, [jnp.stack(s_list) for s_list in new]


def setup_inputs(seed: int = 0) -> dict:
    key = jax.random.key(seed)
    ks = iter(jax.random.split(key, 48))
    nrm = lambda shape, scale: jax.random.normal(next(ks), shape, jnp.float32) * scale
    gain = lambda shape: 1.0 + nrm(shape, 0.02)
    return {
        'x_prompt': nrm((BATCH, SEQ, D_MODEL), 1.0),
        'x_sample': nrm((DEC_BATCH, DEC_SEQ, D_MODEL), 1.0),
        'p_prompt': nrm((DEPTH, BATCH, SEQ, D_PLE), 1.0),
        'p_sample': nrm((DEPTH, DEC_BATCH, DEC_SEQ, D_PLE), 1.0),
        'cache_k': nrm((DEPTH, DEC_BATCH, WINDOW, N_KV_HEADS, HEAD_DIM), 1.0),
        'cache_v': nrm((DEPTH, DEC_BATCH, WINDOW, N_KV_HEADS, HEAD_DIM), 1.0),
        'state_pool': nrm((DEPTH, DEC_BATCH, POOL_STATE, D_POOL), 1.0),
        'state_conv': nrm((DEPTH, DEC_BATCH, CONV_STATE, D_CONV), 0.5),
        'rel_bias': nrm((N_BUCKETS, N_HEADS), 0.5),
        'ln1_g': gain((DEPTH, D_MODEL)),
        'ln1_b': nrm((DEPTH, D_MODEL), 0.02),
        'ffn1_w_gate': nrm((DEPTH, D_MODEL, D_FF), D_MODEL ** -0.5),
        'ffn1_w_up': nrm((DEPTH, D_MODEL, D_FF), D_MODEL ** -0.5),
        'ffn1_w_down': nrm((DEPTH, D_FF, D_MODEL), BETA * D_FF ** -0.5),
        'w_in': nrm((DEPTH, D_MODEL, D_IN), D_MODEL ** -0.5),
        'w_pool': nrm((DEPTH, N_POOL_GROUPS, POOL_GROUP, POOL_GROUP), POOL_GROUP ** -0.5),
        'pool_scale': gain((DEPTH, D_POOL)),
        'w_dw': nrm((DEPTH, CONV_WIDTH, D_CONV), CONV_WIDTH ** -0.5),
        'b_dw': nrm((DEPTH, D_CONV), 0.01),
        'conv_ln_g': gain((DEPTH, D_CONV)),
        'conv_ln_b': nrm((DEPTH, D_CONV), 0.02),
        'w_pw': nrm((DEPTH, D_CONV, D_CONV), D_CONV ** -0.5),
        'sinks': nrm((DEPTH, N_HEADS), 0.5),
        'w_out': nrm((DEPTH, D_MIX, D_MODEL), BETA * D_MIX ** -0.5),
        'ln2_g': gain((DEPTH, D_MODEL)),
        'ln2_b': nrm((DEPTH, D_MODEL), 0.02),
        'ffn2_w_gate': nrm((DEPTH, D_MODEL, D_FF), D_MODEL ** -0.5),
        'ffn2_w_up': nrm((DEPTH, D_MODEL, D_FF), D_MODEL ** -0.5),
        'ffn2_w_down': nrm((DEPTH, D_FF, D_MODEL), BETA * D_FF ** -0.5),
        'ln3_g': gain((DEPTH, D_MODEL)),
        'ln3_b': nrm((DEPTH, D_MODEL), 0.02),
        'w_ple_gate': nrm((DEPTH, D_MODEL, D_MODEL), D_MODEL ** -0.5),
        'w_ple': nrm((DEPTH, D_PLE, D_MODEL), D_PLE ** -0.5),
    }


def reference(x_prompt, x_sample, p_prompt, p_sample, cache_k, cache_v, state_pool, state_conv,
              rel_bias, ln1_g, ln1_b, ffn1_w_gate, ffn1_w_up, ffn1_w_down, w_in, w_pool,
              pool_scale, w_dw, b_dw, conv_ln_g, conv_ln_b, w_pw, sinks, w_out, ln2_g, ln2_b,
              ffn2_w_gate, ffn2_w_up, ffn2_w_down, ln3_g, ln3_b, w_ple_gate, w_ple):
    prm = dict(rel_bias=rel_bias, ln1_g=ln1_g, ln1_b=ln1_b, ffn1_w_gate=ffn1_w_gate,
               ffn1_w_up=ffn1_w_up, ffn1_w_down=ffn1_w_down, w_in=w_in, w_pool=w_pool,
               pool_scale=pool_scale, w_dw=w_dw, b_dw=b_dw, conv_ln_g=conv_ln_g,
               conv_ln_b=conv_ln_b, w_pw=w_pw, sinks=sinks, w_out=w_out, ln2_g=ln2_g,
               ln2_b=ln2_b, ffn2_w_gate=ffn2_w_gate, ffn2_w_up=ffn2_w_up,
               ffn2_w_down=ffn2_w_down, ln3_g=ln3_g, ln3_b=ln3_b, w_ple_gate=w_ple_gate,
               w_ple=w_ple)
    y_prompt, (new_k_prompt, new_v_prompt, new_pool_prompt, new_conv_prompt) = _trunk(
        x_prompt, p_prompt, 0, None, None, None, None, prm)
    y_sample, (new_k_sample, new_v_sample, new_pool_sample, new_conv_sample) = _trunk(
        x_sample, p_sample, PAST_LEN, cache_k, cache_v, state_pool, state_conv, prm)
    return (y_prompt, y_sample, new_k_prompt, new_v_prompt, new_pool_prompt, new_conv_prompt,
            new_k_sample, new_v_sample, new_pool_sample, new_conv_sample)
```

```python
import math
import os
import numpy as np
import concourse.bass as bass
import concourse.mybir as mybir
from concourse.bass_utils import run_bass_kernel_spmd

F32 = mybir.dt.float32
BF16 = mybir.dt.bfloat16
AF = mybir.ActivationFunctionType
ALU = mybir.AluOpType
AX = mybir.AxisListType

D = 2048
DEPTH = 4
NPT = 1024
NS_TOK = 4
T = NPT + NS_TOK
CGS = [(0, 343), (343, 686), (686, 1028)]
DFF = 5632
NFF = DFF // 128
G = 4
NGRP = NFF // G
ALPHA = (2 * DEPTH) ** 0.25
LN_EPS = 1e-5
SCALE = 64 ** -0.5
UW = 2048
NSTG = 2
NWB = 6
LOOKA = 3
ARENA_W = 16680
GW = 384

PC_LN = 0
PC_PS = 96
PC_BDW = 100
PC_CLG = 104
PC_CLB = 108
PC_WDW = 112
PC_SINK = 236
PC_N = 252


class _Rec:
    def __getattr__(self, name):
        def f(*a, **k):
            return (name, a, k)
        return f


R = _Rec()


class Buf:
    __slots__ = ("w", "rs", "name")

    def __init__(self, name=""):
        self.w = None
        self.rs = []
        self.name = name


class Stream:
    def __init__(self, sem, inc=16):
        self.sem = sem
        self.inc = inc
        self.count = 0


class Op:
    __slots__ = ("eng", "fn", "waits", "signal", "idx", "stream", "seq", "semval")


class Prog:
    ENGS = ["pe", "act", "dve", "pool", "sp"]

    def __init__(self):
        self.ops = {e: [] for e in self.ENGS}
        self.seen = {e: {} for e in self.ENGS}
        self.sem = {}

    def op(self, eng, fn, reads=(), writes=(), stream=None):
        o = Op()
        o.eng = eng
        o.fn = fn
        o.waits = []
        o.signal = False
        o.idx = len(self.ops[eng])
        o.stream = stream
        o.seq = 0
        o.semval = 0
        if stream is not None:
            o.seq = stream.count
            stream.count += 1
        deps = []
        for b in reads:
            if b.w is not None:
                deps.append(b.w)
        for b in writes:
            if b.w is not None:
                deps.append(b.w)
            deps.extend(b.rs)
        seen = self.seen[eng]
        best = {}
        for d in deps:
            if d.stream is None:
                if d.eng == eng and eng == "pe":
                    continue
                key = d.eng
                val = d.idx
            else:
                key = d.stream
                val = d.seq
            if seen.get(key, -1) >= val:
                continue
            if key not in best or best[key][0] < val:
                best[key] = (val, d)
        for key, (val, d) in best.items():
            seen[key] = val
            d.signal = True
            o.waits.append(d)
        for b in writes:
            b.w = o
            b.rs = []
        for b in reads:
            if b.w is not o:
                b.rs.append(o)
        self.ops[eng].append(o)
        return o

    def emit(self, engines, final_streams):
        for e in self.ENGS:
            cnt = 0
            for o in self.ops[e]:
                if o.stream is None and o.signal:
                    cnt += 1
                    o.semval = cnt
        for e in self.ENGS:
            eng = engines[e]
            for o in self.ops[e]:
                for d in o.waits:
                    if d.stream is not None:
                        eng.wait_ge(d.stream.sem, d.stream.inc * (d.seq + 1))
                    else:
                        eng.wait_ge(self.sem[d.eng], d.semval)
                ins = getattr(eng, o.fn[0])(*o.fn[1], **o.fn[2])
                if o.stream is not None:
                    if o.stream.inc == 16:
                        ins.then_inc(o.stream.sem, 16)
                    else:
                        ins.then_inc(o.stream.sem)
                elif o.signal:
                    ins.then_inc(self.sem[e], 1)
            if e == "sp":
                for s in final_streams:
                    if s.count > 0:
                        eng.wait_ge(s.sem, s.inc * s.count)


def _bucket_table():
    n = np.arange(128)
    nf = np.maximum(n, 1).astype(np.float32)
    logb = 16 + (np.log(nf / np.float32(16)) / np.float32(math.log(8.0)) * np.float32(16)).astype(np.int32)
    return np.where(n < 16, n, np.minimum(logb, 31))


def build(n_layers=DEPTH, nunits_hint=None, stop_stage=None):
    nc = bass.Bass("TRN2", target_bir_lowering=False)
    P = Prog()
    specs = []

    def din(name, shape, dt=F32):
        return nc.dram_tensor(name, list(shape), dt, kind="ExternalInput").ap()

    def dout(name, shape, dt=F32):
        return nc.dram_tensor(name, list(shape), dt, kind="ExternalOutput").ap()

    NU_ALLOC = nunits_hint if nunits_hint is not None else 1
    xT = din("xT", [16, 128, T])
    pT = din("pT", [DEPTH, 2, 128, T])
    wstream = din("wstream", [NU_ALLOC, 128, UW])
    params = din("params", [128, DEPTH * PC_N])
    ck = din("ck", [DEPTH, NS_TOK, 128, 256])
    cv = din("cv", [DEPTH, NS_TOK, 128, 256])
    spool = din("spool", [DEPTH, NS_TOK, 15, 512])
    sconv = din("sconv", [DEPTH, NS_TOK, 30, 512])
    relaug = din("relaug", [33, 16])
    ohaug = din("ohaug", [33, GW])
    cst = din("cst", [128, 128 + 16 + 64])
    yT = dout("yT", [16, 128, T])
    okp = dout("okp", [DEPTH, 128, 256])
    ovp = dout("ovp", [DEPTH, 128, 256])
    opp = dout("opp", [DEPTH, 15, 512])
    ocp = dout("ocp", [DEPTH, 30, 512])
    oks = dout("oks", [DEPTH, NS_TOK, 128, 256])
    ovs = dout("ovs", [DEPTH, NS_TOK, 128, 256])
    ops_ = dout("ops", [DEPTH, NS_TOK, 15, 512])
    ocs = dout("ocs", [DEPTH, NS_TOK, 30, 512])
    gd = nc.dram_tensor("gd", [16, GW], F32)
    wsk = nc.dram_tensor("wsk", [16, 128, GW], F32)
    HW = 512 + 256 + 512 + 512
    send = [nc.dram_tensor(f"send{l}", [128, HW], F32) for l in range(n_layers)]
    recv = [nc.dram_tensor(f"recv{l}", [256, HW], F32) for l in range(n_layers)]

    import contextlib
    with contextlib.ExitStack() as es:
        def sb(name, shape, dt):
            return es.enter_context(nc.sbuf_tensor(name, list(shape), dt))

        h = sb("h", [128, 16, T], F32)
        hb = sb("hb", [128, 16, T], BF16)
        stg = sb("stg", [128, NSTG, UW], F32)
        wbf = sb("wbf", [128, NWB, UW], BF16)
        par = sb("par", [128, DEPTH * PC_N], F32)
        par2 = sb("par2", [128, DEPTH * 64], F32)
        cs = sb("cs", [128, 128 + 16 + 64], F32)
        identb = sb("identb", [128, 128], BF16)
        onesb = sb("onesb", [128, 128], BF16)
        arena = sb("arena", [128, ARENA_W], F32)
        ps = es.enter_context(nc.psum_tensor("ps", [128, 8, 512], F32))
        psb = ps.bitcast(BF16) if hasattr(ps, "bitcast") else ps[:, :, :].bitcast(BF16)

        def sem(name):
            return es.enter_context(nc.semaphore(name))

        for e in Prog.ENGS:
            P.sem[e] = sem("s_" + e)
        st_init = Stream(sem("st_init"))
        st_stg = [Stream(sem(f"st_stg{i}")) for i in range(NSTG)]
        st_out = Stream(sem("st_out"))
        st_misc = [Stream(sem(f"st_misc{i}")) for i in range(12)]
        st_cc = Stream(sem("st_cc"), inc=1)
        st_x = [Stream(sem(f"st_x{i}")) for i in range(16)]
        _sn = {}

        def ST(name):
            if name not in _sn:
                _sn[name] = Stream(sem("sx_" + name))
            return _sn[name]
        st_out2 = Stream(sem("st_out2"))
        final_streams = [st_out, st_out2]
        block = es.enter_context(nc.Block())

        identf = cs[:, 0:128]
        flag = cs[:, 128:129]
        negflag = cs[:, 129:130]
        invc = cs[:, 144:208]

        Hb = [[Buf(f"h{k}_{c}") for c in range(3)] for k in range(16)]
        HBb = [[Buf(f"hb{k}_{c}") for c in range(3)] for k in range(16)]
        STG = [Buf(f"stg{i}") for i in range(NSTG)]
        WB = [Buf(f"wb{i}") for i in range(NWB)]
        BANK = [Buf(f"bank{i}") for i in range(8)]
        PARB = Buf("par")
        CONSTB = Buf("const")
        arena_bufs = []
        fence = [None]

        def abuf(name):
            b = Buf(name)
            b.w = fence[0]
            arena_bufs.append(b)
            return b

        scratch1 = sb("scr1", [128, 4], F32)

        def arena_reset():
            bl = list(arena_bufs)
            o = P.op("dve", R.memset(scratch1[:, 0:1], 0.0), writes=bl)
            fence[0] = o
            arena_bufs.clear()
            apos[0] = 0

        apos = [0]

        def aalloc(nwords, dt, shape):
            off = apos[0]
            apos[0] += nwords
            assert apos[0] <= ARENA_W, f"arena overflow {apos[0]}"
            v = arena[:, off:off + nwords]
            if dt is BF16:
                v = v.bitcast(BF16)
            if shape is None:
                return v
            if len(shape) == 2:
                return v.rearrange("p (a b) -> p a b", a=shape[0])
            if len(shape) == 3:
                return v.rearrange("p (a b c) -> p a b c", a=shape[0], b=shape[1])
            return v

        issued = {"dma": 0, "cast": 0}

        def issue_dma(i):
            s = i % NSTG
            P.op("sp", R.dma_start(out=stg[:, s, :], in_=wstream[min(i, NU_ALLOC - 1)]),
                 writes=[STG[s]], stream=st_stg[s])

        def issue_cast(i):
            s = i % NSTG
            w = i % NWB
            if i % 2 == 0:
                P.op("act", R.activation(out=wbf[:, w, :], in_=stg[:, s, :], func=AF.Copy),
                     reads=[STG[s]], writes=[WB[w]])
            else:
                P.op("pool", R.tensor_copy(out=wbf[:, w, :], in_=stg[:, s, :]),
                     reads=[STG[s]], writes=[WB[w]])

        def fetch(spec):
            n = len(specs)
            specs.append(spec)
            while issued["cast"] <= n + LOOKA - 1:
                i = issued["cast"]
                while issued["dma"] <= i:
                    issue_dma(issued["dma"])
                    issued["dma"] += 1
                issue_cast(i)
                issued["cast"] += 1
            return n % NWB

        rot = {"gu": 0, "dn": 0}

        def mm(out, lhsT, rhs, start, stop, reads, writes):
            return P.op("pe", R.matmul(out, lhsT=lhsT, rhs=rhs, start=start, stop=stop),
                        reads=reads, writes=writes)

        P.op("pool", R.dma_start(out=par[:, :], in_=params), writes=[PARB], stream=st_init)
        P.op("pool", R.dma_start(out=cs[:, :], in_=cst), writes=[CONSTB], stream=st_init)
        for k in range(16):
            P.op("pool", R.dma_start(out=h[:, k, :], in_=xT[k]),
                 writes=[Hb[k][0], Hb[k][1], Hb[k][2]], stream=st_x[k])
        P.op("dve", R.tensor_copy(out=identb[:, :], in_=identf), reads=[CONSTB], writes=[CONSTB])
        P.op("dve", R.memset(onesb[:, :], 1.0), writes=[CONSTB])
        for l in range(n_layers):
            src = par[:, l * PC_N:l * PC_N + 64]
            P.op("dve", R.tensor_scalar(out=par2[:, l * 64:(l + 1) * 64], in0=src,
                                                                  scalar1=float(ALPHA), scalar2=None, op0=ALU.mult),
                 reads=[PARB], writes=[PARB])
        for k in range(16):
            for c, (c0, c1) in enumerate(CGS):
                P.op("act", R.activation(out=hb[:, k, c0:c1], in_=h[:, k, c0:c1], func=AF.Copy),
                     reads=[Hb[k][c]], writes=[HBb[k][c]])
                P.op("dve", R.tensor_scalar(out=h[:, k, c0:c1], in0=h[:, k, c0:c1],
                                                                          scalar1=float(ALPHA), scalar2=None, op0=ALU.mult),
                     reads=[Hb[k][c]], writes=[Hb[k][c]])

        BIASB = Buf("biasdram")
        arena_reset()
        ra = aalloc(16, F32, None)
        oh = aalloc(GW, F32, None)
        gsb = aalloc(GW, F32, None)
        b_ra = abuf("ra")
        b_gsb = abuf("gsb")
        P.op("pool", R.dma_start(out=ra[0:33, :], in_=relaug), writes=[b_ra], stream=ST("ra"))
        P.op("pool", R.dma_start(out=oh[0:33, :], in_=ohaug), writes=[b_ra], stream=ST("oh"))
        P.op("pe", R.matmul(ps[0:16, 0, 0:GW], lhsT=ra[0:33, :], rhs=oh[0:33, :], start=True, stop=True),
             reads=[b_ra], writes=[BANK[0]])
        P.op("dve", R.tensor_copy(out=gsb[0:16, :], in_=ps[0:16, 0, 0:GW]), reads=[BANK[0]], writes=[b_gsb])
        P.op("pool", R.dma_start(out=gd[:, :], in_=gsb[0:16, :]), reads=[b_gsb], writes=[BIASB], stream=ST("gd"))
        P.op("pool", R.dma_start(out=wsk[:, :, :], in_=bass.AP(gd.ap().tensor, 0, [[GW, 16], [0, 128], [1, GW]])),
             reads=[BIASB], writes=[BIASB], stream=ST("wsk"))

        def layer_norm(l, which, scaled):
            arena_reset()
            pb = l * PC_N + PC_LN + which * 32
            g_ap = lambda k: par[:, pb + k:pb + k + 1]
            b_ap = lambda k: par[:, pb + 16 + k:pb + 17 + k]
            if scaled:
                q = l * 64 + which * 32
                gs_ap = lambda k: par2[:, q + k:q + k + 1]
                bs_ap = lambda k: par2[:, q + 16 + k:q + 17 + k]
            else:
                gs_ap, bs_ap = g_ap, b_ap
            rb = aalloc(344, BF16, [2, 344])
            rsq = aalloc(344, BF16, [2, 344])
            mean = aalloc(344, F32, None)
            var = aalloc(344, F32, None)
            rstd = aalloc(344, F32, None)
            t1 = aalloc(688, F32, [2, 344])
            t2 = aalloc(688, F32, [2, 344])
            b_rb = [abuf("rb0"), abuf("rb1")]
            b_rsq = [abuf("rsq0"), abuf("rsq1")]
            b_st = abuf("stats")
            b_t1 = [abuf("t10"), abuf("t11")]
            b_t2 = [abuf("t20"), abuf("t21")]
            for c, (c0, c1) in enumerate(CGS):
                n = c1 - c0
                bk1, bk2 = 4 + 2 * (c % 2), 5 + 2 * (c % 2)
                for k in range(16):
                    i = k % 2
                    P.op("act", R.activation(out=rb[:, i, 0:n], in_=h[:, k, c0:c1], func=AF.Copy),
                         reads=[Hb[k][c]], writes=[b_rb[i]])
                    P.op("act", R.activation(out=rsq[:, i, 0:n], in_=h[:, k, c0:c1], func=AF.Square),
                         reads=[Hb[k][c]], writes=[b_rsq[i]])
                    mm(ps[:, bk1, 0:n], onesb[:, :], rb[:, i, 0:n], k == 0, k == 15, [b_rb[i], CONSTB], [BANK[bk1]])
                    mm(ps[:, bk2, 0:n], onesb[:, :], rsq[:, i, 0:n], k == 0, k == 15, [b_rsq[i], CONSTB], [BANK[bk2]])
                P.op("dve", R.tensor_scalar(out=mean[:, 0:n], in0=ps[:, bk1, 0:n], scalar1=1.0 / D, scalar2=None, op0=ALU.mult),
                     reads=[BANK[bk1]], writes=[b_st])
                P.op("dve", R.tensor_tensor(out=var[:, 0:n], in0=mean[:, 0:n], in1=mean[:, 0:n], op=ALU.mult),
                     reads=[b_st], writes=[b_st])
                P.op("dve", R.scalar_tensor_tensor(out=var[:, 0:n], in0=ps[:, bk2, 0:n], scalar=1.0 / D, in1=var[:, 0:n],
                                                             op0=ALU.mult, op1=ALU.subtract),
                     reads=[BANK[bk2], b_st], writes=[b_st])
                P.op("dve", R.tensor_scalar(out=var[:, 0:n], in0=var[:, 0:n], scalar1=float(LN_EPS), scalar2=None, op0=ALU.add),
                     reads=[b_st], writes=[b_st])
                P.op("act", R.activation(out=var[:, 0:n], in_=var[:, 0:n], func=AF.Sqrt), reads=[b_st], writes=[b_st])
                P.op("dve", R.reciprocal(out=rstd[:, 0:n], in_=var[:, 0:n]), reads=[b_st], writes=[b_st])
                for k in range(16):
                    i = k % 2
                    P.op("pool", R.tensor_tensor(out=t1[:, i, 0:n], in0=h[:, k, c0:c1], in1=mean[:, 0:n], op=ALU.subtract),
                         reads=[Hb[k][c], b_st], writes=[b_t1[i]])
                    P.op("dve", R.tensor_tensor(out=t2[:, i, 0:n], in0=t1[:, i, 0:n], in1=rstd[:, 0:n], op=ALU.mult),
                         reads=[b_t1[i], b_st], writes=[b_t2[i]])
                    P.op("act", R.activation(out=h[:, k, c0:c1], in_=t2[:, i, 0:n], func=AF.Identity,
                                                                 scale=gs_ap(k), bias=bs_ap(k)),
                         reads=[b_t2[i], PARB], writes=[Hb[k][c]])
                    P.op("act", R.activation(out=hb[:, k, c0:c1], in_=t2[:, i, 0:n], func=AF.Identity,
                                                                 scale=g_ap(k), bias=b_ap(k)),
                         reads=[b_t2[i], PARB], writes=[HBb[k][c]])

        def ffn(l, which):
            arena_reset()
            act = aalloc(2 * G * 514, BF16, [2 * G, T])
            sg = aalloc(688, F32, [2, 344])
            b_act = [[[abuf(f"act{p}_{j}_{c}") for c in range(3)] for j in range(G)] for p in range(2)]
            b_sg = [abuf("sg0"), abuf("sg1")]
            cnt = {"gu": 0, "dn": 0, "sg": 0}

            def gateup(grp):
                p = grp % 2
                for jj in range(G):
                    j = grp * G + jj
                    wg = fetch(("gate", l, which, j))
                    wu = fetch(("up", l, which, j))
                    for c, (c0, c1) in enumerate(CGS):
                        n = c1 - c0
                        pair = cnt["gu"] % 2
                        cnt["gu"] += 1
                        bg, bu = 2 * pair, 2 * pair + 1
                        for k in range(16):
                            mm(ps[:, bg, 0:n], wbf[:, wg, k * 128:(k + 1) * 128], hb[:, k, c0:c1], k == 0, k == 15,
                               [WB[wg], HBb[k][c]], [BANK[bg]])
                        for k in range(16):
                            mm(ps[:, bu, 0:n], wbf[:, wu, k * 128:(k + 1) * 128], hb[:, k, c0:c1], k == 0, k == 15,
                               [WB[wu], HBb[k][c]], [BANK[bu]])
                        si = cnt["sg"] % 2
                        cnt["sg"] += 1
                        P.op("act", R.activation(out=sg[:, si, 0:n], in_=ps[:, bg, 0:n], func=AF.Silu),
                             reads=[BANK[bg]], writes=[b_sg[si]])
                        P.op("dve", R.tensor_tensor(out=act[:, p * G + jj, c0:c1], in0=sg[:, si, 0:n], in1=ps[:, bu, 0:n], op=ALU.mult),
                             reads=[b_sg[si], BANK[bu]], writes=[b_act[p][jj][c]])

            def down(grp):
                p = grp % 2
                wd = [fetch(("down", l, which, grp * G + jj)) for jj in range(G)]
                for dq in range(16):
                    for c, (c0, c1) in enumerate(CGS):
                        n = c1 - c0
                        bk = 4 + cnt["dn"] % 4
                        cnt["dn"] += 1
                        for jj in range(G):
                            mm(ps[:, bk, 0:n], wbf[:, wd[jj], dq * 128:(dq + 1) * 128], act[:, p * G + jj, c0:c1],
                               jj == 0, jj == G - 1, [WB[wd[jj]], b_act[p][jj][c]], [BANK[bk]])
                        P.op("dve", R.scalar_tensor_tensor(out=h[:, dq, c0:c1], in0=ps[:, bk, 0:n], scalar=0.5, in1=h[:, dq, c0:c1],
                                                    op0=ALU.mult, op1=ALU.add),
                             reads=[BANK[bk], Hb[dq][c]], writes=[Hb[dq][c]])

            gateup(0)
            for grp in range(NGRP):
                if grp + 1 < NGRP:
                    gateup(grp + 1)
                down(grp)

        def ple(l, last):
            arena_reset()
            pf = aalloc(2 * T, F32, [2, T])
            pbf = aalloc(T, BF16, [2, T])
            sgt = aalloc(688, F32, [2, 344])
            b_pf = abuf("pf")
            b_pb = abuf("pb")
            b_s = [abuf("s0"), abuf("s1")]
            for kk in range(2):
                P.op("pool", R.dma_start(out=pf[:, kk, :], in_=pT[l, kk]), writes=[b_pf], stream=ST(f"pT{kk}"))
            P.op("dve", R.tensor_copy(out=pbf[:, :, :], in_=pf[:, :, :]), reads=[b_pf], writes=[b_pb])
            wple = aalloc(2 * UW // 2, BF16, [2, UW])
            b_wple = abuf("wple")
            for i in range(2):
                wsl = fetch(("ple", l, i))
                P.op("dve", R.tensor_copy(out=wple[:, i, :], in_=wbf[:, wsl, :]), reads=[WB[wsl]], writes=[b_wple])
            cnt = 0
            for dq in range(16):
                wg = fetch(("pg", l, dq))
                for c, (c0, c1) in enumerate(CGS):
                    n = c1 - c0
                    pair = cnt % 2
                    cnt += 1
                    bg, bu = 2 * pair, 2 * pair + 1
                    for k in range(16):
                        mm(ps[:, bg, 0:n], wbf[:, wg, k * 128:(k + 1) * 128], hb[:, k, c0:c1], k == 0, k == 15,
                           [WB[wg], HBb[k][c]], [BANK[bg]])
                    wpi = dq // 8
                    off = (dq % 8) * 128
                    for kk in range(2):
                        mm(ps[:, bu, 0:n], wple[:, wpi, kk * 1024 + off:kk * 1024 + off + 128], pbf[:, kk, c0:c1], kk == 0, kk == 1,
                           [b_wple, b_pb], [BANK[bu]])
                    si = pair
                    P.op("act", R.activation(out=sgt[:, si, 0:n], in_=ps[:, bg, 0:n], func=AF.Sigmoid),
                         reads=[BANK[bg]], writes=[b_s[si]])
                    P.op("dve", R.tensor_tensor(out=sgt[:, si, 0:n], in0=sgt[:, si, 0:n], in1=ps[:, bu, 0:n], op=ALU.mult),
                         reads=[b_s[si], BANK[bu]], writes=[b_s[si]])
                    P.op("pool", R.tensor_tensor(out=h[:, dq, c0:c1], in0=h[:, dq, c0:c1], in1=sgt[:, si, 0:n], op=ALU.add),
                         reads=[b_s[si], Hb[dq][c]], writes=[Hb[dq][c]])
            for k in range(16):
                for c, (c0, c1) in enumerate(CGS):
                    if last:
                        continue
                    P.op("act", R.activation(out=hb[:, k, c0:c1], in_=h[:, k, c0:c1], func=AF.Copy),
                         reads=[Hb[k][c]], writes=[HBb[k][c]])
                    P.op("dve", R.tensor_scalar(out=h[:, k, c0:c1], in0=h[:, k, c0:c1],
                                                                              scalar1=float(ALPHA), scalar2=None, op0=ALU.mult),
                         reads=[Hb[k][c]], writes=[Hb[k][c]])

        def proj_fm(wslot, evac):
            for c, (c0, c1) in enumerate(CGS):
                n = c1 - c0
                bk = rot["gu"] % 4
                rot["gu"] += 1
                for k in range(16):
                    mm(ps[:, bk, 0:n], wbf[:, wslot, k * 128:(k + 1) * 128], hb[:, k, c0:c1], k == 0, k == 15,
                       [WB[wslot], HBb[k][c]], [BANK[bk]])
                evac(c, c0, c1, n, bk)

        def proj_tm(wslot, t0, nt, bk, col0):
            cs_ = [c for c, (a0, a1) in enumerate(CGS) if a0 < t0 + nt and a1 > t0]
            for k in range(16):
                mm(ps[0:nt, bk, col0:col0 + 128], hb[:, k, t0:t0 + nt], wbf[:, wslot, k * 128:(k + 1) * 128], k == 0, k == 15,
                   [WB[wslot]] + [HBb[k][c] for c in cs_], [BANK[bk]])

        def wout_group(l, grp, mtg, b_mtg):
            wo = [fetch(("wout", l, grp * 4 + jj)) for jj in range(4)]
            for dq in range(16):
                for c, (c0, c1) in enumerate(CGS):
                    n = c1 - c0
                    bk = 4 + rot["dn"] % 4
                    rot["dn"] += 1
                    for jj in range(4):
                        mm(ps[:, bk, 0:n], wbf[:, wo[jj], dq * 128:(dq + 1) * 128], mtg[:, jj, c0:c1], jj == 0, jj == 3,
                           [WB[wo[jj]], b_mtg[jj][c]], [BANK[bk]])
                    P.op("dve", R.tensor_tensor(out=h[:, dq, c0:c1], in0=ps[:, bk, 0:n], in1=h[:, dq, c0:c1], op=ALU.add),
                         reads=[BANK[bk], Hb[dq][c]], writes=[Hb[dq][c]])

        def mixer(l):
            pl = l * PC_N
            TK, TV, TU, TC = 0, 512, 768, 1280
            t0p, t0s = NPT - 128, NPT
            sink_ap = lambda hh: par[:, pl + PC_SINK + hh:pl + PC_SINK + hh + 1]
            arena_reset()
            hal = aalloc(HW, F32, None)
            tl = aalloc(HW, F32, None)
            mtg = aalloc(2 * T, BF16, [4, T])
            b_hal = abuf("hal")
            b_tl = abuf("tl")
            b_mtg = [[abuf(f"mtg{j}_{c}") for c in range(3)] for j in range(4)]
            base_pos = apos[0]
            sgt = aalloc(1024, F32, None)
            b_sgt = abuf("sgt")
            tls = hal
            b_tls = b_hal

            for ci in range(4):
                w = fetch(("in", l, "a", ci))
                proj_tm(w, t0p, 128, 0, ci * 128)
                proj_tm(w, t0s, 4, 1, ci * 128)
            P.op("act", R.activation(out=tl[:, TC:TC + 512], in_=ps[:, 0, :], func=AF.Copy), reads=[BANK[0]], writes=[b_tl])
            P.op("act", R.activation(out=tls[0:4, TC:TC + 512], in_=ps[0:4, 1, :], func=AF.Copy), reads=[BANK[1]], writes=[b_tls])
            for ci in range(4):
                w = fetch(("in", l, "g", ci))
                proj_tm(w, t0p, 128, 2, ci * 128)
                proj_tm(w, t0s, 4, 3, ci * 128)
            P.op("act", R.activation(out=sgt[:, 0:512], in_=ps[:, 2, :], func=AF.Sigmoid), reads=[BANK[2]], writes=[b_sgt])
            P.op("act", R.activation(out=sgt[0:4, 512:1024], in_=ps[0:4, 3, :], func=AF.Sigmoid), reads=[BANK[3]], writes=[b_sgt])
            P.op("dve", R.tensor_tensor(out=tl[:, TC:TC + 512], in0=tl[:, TC:TC + 512], in1=sgt[:, 0:512], op=ALU.mult),
                 reads=[b_tl, b_sgt], writes=[b_tl])
            P.op("dve", R.tensor_tensor(out=tls[0:4, TC:TC + 512], in0=tls[0:4, TC:TC + 512], in1=sgt[0:4, 512:1024], op=ALU.mult),
                 reads=[b_tls, b_sgt], writes=[b_tls])
            for ci in range(4):
                w = fetch(("in", l, "u", ci))
                proj_tm(w, t0p, 128, 0, ci * 128)
                proj_tm(w, t0s, 4, 1, ci * 128)
            P.op("act", R.activation(out=tl[:, TU:TU + 512], in_=ps[:, 0, :], func=AF.Copy), reads=[BANK[0]], writes=[b_tl])
            P.op("act", R.activation(out=tls[0:4, TU:TU + 512], in_=ps[0:4, 1, :], func=AF.Copy), reads=[BANK[1]], writes=[b_tls])
            for g in range(4):
                w = fetch(("in", l, "kd", g))
                proj_tm(w, t0p, 128, 2, g * 128)
                proj_tm(w, t0s, 4, 3, g * 128)
            P.op("dve", R.tensor_copy(out=tl[:, TK:TK + 512], in_=ps[:, 2, :]), reads=[BANK[2]], writes=[b_tl])
            P.op("dve", R.tensor_copy(out=tls[0:4, TK:TK + 512], in_=ps[0:4, 3, :]), reads=[BANK[3]], writes=[b_tls])
            for vu in range(2):
                w = fetch(("in", l, "v", vu))
                proj_tm(w, t0p, 128, 0, vu * 128)
                proj_tm(w, t0s, 4, 1, vu * 128)
            P.op("act", R.activation(out=tl[:, TV:TV + 256], in_=ps[:, 0, 0:256], func=AF.Copy), reads=[BANK[0]], writes=[b_tl])
            P.op("act", R.activation(out=tls[0:4, TV:TV + 256], in_=ps[0:4, 1, 0:256], func=AF.Copy), reads=[BANK[1]], writes=[b_tls])
            kt = int(os.environ.get("KT", "9"))
            if kt <= 1:
                return
            tlk = tl[:, TK:TK + 512].rearrange("p (g r d) -> p g r d", g=4, r=2)[:, :, 0, :]
            P.op("pool", R.dma_start(out=okp[l].rearrange("p (g d) -> p g d", g=4), in_=tlk), reads=[b_tl], stream=ST("tl"))
            P.op("pool", R.dma_start(out=ovp[l], in_=tl[:, TV:TV + 256]), reads=[b_tl], stream=ST("tl"))
            P.op("pool", R.dma_start(out=opp[l], in_=tl[113:128, TU:TU + 512]), reads=[b_tl], stream=ST("tl"))
            P.op("pool", R.dma_start(out=ocp[l], in_=tl[98:128, TC:TC + 512]), reads=[b_tl], stream=ST("tl"))
            SB_ = Buf("send")
            RB_ = Buf("recv")
            P.op("pool", R.dma_start(out=send[l][:, :], in_=tl[:, :]), reads=[b_tl], writes=[SB_], stream=ST("tl"))
            if kt <= 2:
                return
            P.op("pool", R.collective_compute("AllGather", ALU.bypass, replica_groups=[[0, 1], [2, 3], [4, 5], [6, 7]],
                                                        ins=[send[l].ap().opt()], outs=[recv[l].ap().opt()]),
                 reads=[SB_], writes=[RB_], stream=st_cc)
            if kt <= 3:
                return
            OKS = Buf("oks")
            for b in range(NS_TOK):
                P.op("pool", R.dma_start(out=oks[l, b, 0:127, :], in_=ck[l, b, 1:128, :]), writes=[OKS], stream=ST("oks"))
                P.op("pool", R.dma_start(out=ovs[l, b, 0:127, :], in_=cv[l, b, 1:128, :]), writes=[OKS], stream=ST("oks"))
                P.op("pool", R.dma_start(out=ops_[l, b, 0:14, :], in_=spool[l, b, 1:15, :]), stream=st_out2)
                P.op("pool", R.dma_start(out=ocs[l, b, 0:29, :], in_=sconv[l, b, 1:30, :]), stream=st_out2)
            for b in range(NS_TOK):
                P.op("pool", R.dma_start(out=ops_[l, b, 14:15, :], in_=tls[b:b + 1, TU:TU + 512]), reads=[b_tls], stream=ST("tls"))
                P.op("pool", R.dma_start(out=ocs[l, b, 29:30, :], in_=tls[b:b + 1, TC:TC + 512]), reads=[b_tls], stream=ST("tls"))
            for b in range(NS_TOK):
                tsk = tls[b:b + 1, TK:TK + 512].rearrange("p (g r d) -> p g r d", g=4, r=2)[:, :, 0, :]
                P.op("pool", R.dma_start(out=oks[l, b, 127:128, :].rearrange("p (g d) -> p g d", g=4), in_=tsk),
                     reads=[b_tls], writes=[OKS], stream=ST("oks"))
                P.op("pool", R.dma_start(out=ovs[l, b, 127:128, :], in_=tls[b:b + 1, TV:TV + 256]), reads=[b_tls], writes=[OKS], stream=ST("oks"))
            if kt <= 4:
                return
            P.op("pool", R.dma_start(out=hal[:, :], in_=recv[l][0:128, :]), reads=[RB_], writes=[b_hal], stream=ST("hal"))

            kmix = int(os.environ.get("KMIX", "9"))
            if kmix <= 1:
                return
            apos[0] = base_pos
            P.op("dve", R.memset(scratch1[:, 1:2], 0.0), writes=[b_sgt])
            f1 = P.ops["dve"][-1]

            def nbf(name, f):
                b_ = Buf(name)
                b_.w = f
                arena_bufs.append(b_)
                return b_
            cin = aalloc(4 * (30 + NPT), F32, [4, 30 + NPT])
            acc = aalloc(NPT, F32, None)
            cins = aalloc(4 * NS_TOK * 31, F32, None)
            cinsv = cins[:, :].rearrange("p (c b j) -> p c b j", c=4, b=NS_TOK)
            ycs = aalloc(4 * NS_TOK, F32, [4, NS_TOK])
            sg2 = aalloc(688, F32, [2, 344])
            scs = aalloc(512, F32, None)
            cvt = aalloc(NS_TOK * 31, F32, None)
            ycb = aalloc(344, BF16, [2, 344])
            ysq = aalloc(344, BF16, [2, 344])
            mean = aalloc(344, F32, None)
            var = aalloc(344, F32, None)
            rstd = aalloc(344, F32, None)
            t1 = aalloc(688, F32, [2, 344])
            ycn = aalloc(4 * 344 // 2, BF16, [4, 344])
            b_cin = [nbf(f"cin{ci}", f1) for ci in range(4)]
            b_acc, b_cinh, b_cins, b_scs, b_cvt, b_st = [nbf(n, f1) for n in ("acc", "cinh", "cins", "scs", "cvt", "cstt")]
            b_sg2 = [nbf("sg20", f1), nbf("sg21", f1)]
            b_ycb = [nbf("ycb0", f1), nbf("ycb1", f1)]
            b_ysq = [nbf("ysq0", f1), nbf("ysq1", f1)]
            b_t1 = [nbf("ct10", f1), nbf("ct11", f1)]
            b_ycn = [nbf(f"ycn{ci}", f1) for ci in range(4)]
            for ci in range(4):
                P.op("pe", R.transpose(ps[:, 2, ci * 128:(ci + 1) * 128], hal[:, TC + ci * 128:TC + (ci + 1) * 128], identf),
                     reads=[b_hal, CONSTB], writes=[BANK[2]])
            P.op("dve", R.tensor_scalar(out=cin[:, :, 0:30], in0=ps[:, 2, :].rearrange("p (c t) -> p c t", c=4)[:, :, 98:128],
                                                  scalar1=flag, scalar2=None, op0=ALU.mult),
                 reads=[BANK[2], CONSTB], writes=[b_cinh])
            P.op("pool", R.dma_start(out=scs[0:120, :], in_=sconv[l].rearrange("b r c -> (b r) c")), writes=[b_scs], stream=ST("scs"))
            for ci in range(4):
                P.op("pe", R.transpose(ps[:, 3, ci * 128:ci * 128 + 120], scs[0:120, ci * 128:(ci + 1) * 128], identf[0:120, 0:120]),
                     reads=[b_scs, CONSTB], writes=[BANK[3]])
            P.op("dve", R.tensor_copy(out=cinsv[:, :, :, 0:30],
                                                in_=ps[:, 3, :].rearrange("p (c x) -> p c x", c=4)[:, :, 0:120].rearrange("p c (b r) -> p c b r", b=NS_TOK)),
                 reads=[BANK[3]], writes=[b_cins])
            for ci in range(4):
                wg_ = fetch(("in", l, "g", ci))
                wa_ = fetch(("in", l, "a", ci))
                for c, (c0, c1) in enumerate(CGS):
                    n = c1 - c0
                    pair = rot["gu"] % 2
                    rot["gu"] += 1
                    bg, ba = 2 * pair, 2 * pair + 1
                    for k in range(16):
                        mm(ps[:, bg, 0:n], wbf[:, wg_, k * 128:(k + 1) * 128], hb[:, k, c0:c1], k == 0, k == 15, [WB[wg_], HBb[k][c]], [BANK[bg]])
                    for k in range(16):
                        mm(ps[:, ba, 0:n], wbf[:, wa_, k * 128:(k + 1) * 128], hb[:, k, c0:c1], k == 0, k == 15, [WB[wa_], HBb[k][c]], [BANK[ba]])
                    si = pair
                    P.op("act", R.activation(out=sg2[:, si, 0:n], in_=ps[:, bg, 0:n], func=AF.Sigmoid),
                         reads=[BANK[bg]], writes=[b_sg2[si]])
                    npr = min(c1, NPT) - c0
                    P.op("dve", R.tensor_tensor(out=cin[:, ci, 30 + c0:30 + c0 + npr], in0=sg2[:, si, 0:npr], in1=ps[:, ba, 0:npr], op=ALU.mult),
                         reads=[b_sg2[si], BANK[ba]], writes=[b_cin[ci]])
                    if c == 2:
                        P.op("dve", R.tensor_tensor(out=cinsv[:, ci, :, 30], in0=sg2[:, si, npr:npr + NS_TOK], in1=ps[:, ba, npr:npr + NS_TOK], op=ALU.mult),
                             reads=[b_sg2[si], BANK[ba]], writes=[b_cins])
            for ci in range(4):
                wj = lambda j, ci=ci: par[:, pl + PC_WDW + ci * 31 + j:pl + PC_WDW + ci * 31 + j + 1]
                bd = par[:, pl + PC_BDW + ci:pl + PC_BDW + ci + 1]
                rd = [b_cin[ci], b_cinh, PARB]
                P.op("dve", R.tensor_scalar(out=acc[:, 0:NPT], in0=cin[:, ci, 0:NPT], scalar1=wj(0), scalar2=bd,
                                                                            op0=ALU.mult, op1=ALU.add), reads=rd, writes=[b_acc])
                for j in range(1, 31):
                    P.op("dve", R.scalar_tensor_tensor(out=acc[:, 0:NPT], in0=cin[:, ci, j:j + NPT], scalar=wj(j),
                                                                                   in1=acc[:, 0:NPT], op0=ALU.mult, op1=ALU.add),
                         reads=rd + [b_acc], writes=[b_acc])
                P.op("pool", R.tensor_copy(out=cin[:, ci, 0:NPT], in_=acc[:, 0:NPT]), reads=[b_acc, b_cinh], writes=[b_cin[ci]])
                wrow = par[:, pl + PC_WDW + ci * 31:pl + PC_WDW + (ci + 1) * 31]
                P.op("dve", R.tensor_tensor(out=cvt[:, :].rearrange("p (b j) -> p b j", b=NS_TOK),
                                                                        in0=cinsv[:, ci, :, :],
                                                                        in1=wrow.unsqueeze(1).to_broadcast([128, NS_TOK, 31]), op=ALU.mult),
                     reads=[b_cins, PARB], writes=[b_cvt])
                P.op("dve", R.tensor_reduce(out=ycs[:, ci, :], in_=cvt[:, :].rearrange("p (b j) -> p b j", b=NS_TOK), axis=AX.X, op=ALU.add),
                     reads=[b_cvt], writes=[b_cins])
                P.op("dve", R.tensor_scalar(out=ycs[:, ci, :], in0=ycs[:, ci, :], scalar1=bd, scalar2=None, op0=ALU.add),
                     reads=[b_cins, PARB], writes=[b_cins])
            wpw = fetch(("pw", l))

            def ycv(ci, c0, c1):
                npr = min(c1, NPT) - c0
                out = [(cin[:, ci, c0:c0 + npr], 0, npr)]
                if c1 > NPT:
                    out.append((ycs[:, ci, :], npr, NS_TOK))
                return out
            for c, (c0, c1) in enumerate(CGS):
                n = c1 - c0
                for ci in range(4):
                    i = ci % 2
                    for (srcap, o0, wd_) in ycv(ci, c0, c1):
                        P.op("act", R.activation(out=ycb[:, i, o0:o0 + wd_], in_=srcap, func=AF.Copy),
                             reads=[b_cin[ci], b_cins], writes=[b_ycb[i]])
                        P.op("act", R.activation(out=ysq[:, i, o0:o0 + wd_], in_=srcap, func=AF.Square),
                             reads=[b_cin[ci], b_cins], writes=[b_ysq[i]])
                    mm(ps[:, 4, 0:n], onesb[:, :], ycb[:, i, 0:n], ci == 0, ci == 3, [b_ycb[i], CONSTB], [BANK[4]])
                    mm(ps[:, 5, 0:n], onesb[:, :], ysq[:, i, 0:n], ci == 0, ci == 3, [b_ysq[i], CONSTB], [BANK[5]])
                P.op("dve", R.tensor_scalar(out=mean[:, 0:n], in0=ps[:, 4, 0:n], scalar1=1.0 / 512, scalar2=None, op0=ALU.mult), reads=[BANK[4]], writes=[b_st])
                P.op("dve", R.tensor_tensor(out=var[:, 0:n], in0=mean[:, 0:n], in1=mean[:, 0:n], op=ALU.mult), reads=[b_st], writes=[b_st])
                P.op("dve", R.scalar_tensor_tensor(out=var[:, 0:n], in0=ps[:, 5, 0:n], scalar=1.0 / 512, in1=var[:, 0:n], op0=ALU.mult, op1=ALU.subtract),
                     reads=[BANK[5], b_st], writes=[b_st])
                P.op("dve", R.tensor_scalar(out=var[:, 0:n], in0=var[:, 0:n], scalar1=float(LN_EPS), scalar2=None, op0=ALU.add),
                     reads=[b_st], writes=[b_st])
                P.op("act", R.activation(out=var[:, 0:n], in_=var[:, 0:n], func=AF.Sqrt), reads=[b_st], writes=[b_st])
                P.op("dve", R.reciprocal(out=rstd[:, 0:n], in_=var[:, 0:n]), reads=[b_st], writes=[b_st])
                for ci in range(4):
                    i = ci % 2
                    gcl = par[:, pl + PC_CLG + ci:pl + PC_CLG + ci + 1]
                    bcl = par[:, pl + PC_CLB + ci:pl + PC_CLB + ci + 1]
                    for (srcap, o0, wd_) in ycv(ci, c0, c1):
                        P.op("pool", R.tensor_tensor(out=t1[:, i, o0:o0 + wd_], in0=srcap, in1=mean[:, o0:o0 + wd_], op=ALU.subtract),
                             reads=[b_cin[ci], b_cins, b_st], writes=[b_t1[i]])
                    P.op("dve", R.tensor_tensor(out=t1[:, i, 0:n], in0=t1[:, i, 0:n], in1=rstd[:, 0:n], op=ALU.mult),
                         reads=[b_t1[i], b_st], writes=[b_t1[i]])
                    P.op("act", R.activation(out=t1[:, i, 0:n], in_=t1[:, i, 0:n], func=AF.Identity, scale=gcl, bias=bcl),
                         reads=[b_t1[i], PARB], writes=[b_t1[i]])
                    P.op("act", R.activation(out=ycn[:, ci, 0:n], in_=t1[:, i, 0:n], func=AF.Silu), reads=[b_t1[i]], writes=[b_ycn[ci]])
                for dj in range(4):
                    bk = rot["gu"] % 4
                    rot["gu"] += 1
                    for ci in range(4):
                        mm(ps[:, bk, 0:n], wbf[:, wpw, ci * 512 + dj * 128:ci * 512 + (dj + 1) * 128], ycn[:, ci, 0:n], ci == 0, ci == 3,
                           [WB[wpw], b_ycn[ci]], [BANK[bk]])
                    P.op("act", R.activation(out=mtg[:, dj, c0:c1], in_=ps[:, bk, 0:n], func=AF.Copy),
                         reads=[BANK[bk]], writes=[b_mtg[dj][c]])
            wout_group(l, 1, mtg, b_mtg)

            if kmix <= 2:
                return
            P.op("dve", R.memset(scratch1[:, 2:3], 0.0),
                 writes=b_cin + [b_acc, b_cinh, b_cins, b_scs, b_cvt, b_st] + b_sg2 + b_ycb + b_ysq + b_t1 + b_ycn)
            f2 = P.ops["dve"][-1]
            apos[0] = base_pos
            L = 15 + NPT
            ue = aalloc(L + 1, F32, None)
            ua = aalloc(L + 1, F32, None)
            ub = aalloc(L + 1, F32, None)
            dd = aalloc(T // 2, BF16, None)
            dds = aalloc(8, F32, None)
            sps = aalloc(512, F32, None)
            uh = aalloc(4 * 16, F32, [4, 16])
            ush = aalloc(4 * NS_TOK * 16, F32, None)
            ushv = ush[:, :].rearrange("p (c b r) -> p c b r", c=4, b=NS_TOK)
            b_ue, b_ua, b_ub, b_dds, b_dd, b_sps, b_uh = [nbf(n, f2) for n in ("ue", "ua", "ub", "dds", "dd", "sps", "uh")]
            for ci in range(4):
                P.op("pe", R.transpose(ps[:, 2, ci * 128:(ci + 1) * 128], hal[:, TU + ci * 128:TU + (ci + 1) * 128], identf),
                     reads=[b_hal, CONSTB], writes=[BANK[2]])
            P.op("dve", R.tensor_scalar(out=uh[:, :, 0:15], in0=ps[:, 2, :].rearrange("p (c t) -> p c t", c=4)[:, :, 113:128],
                                                  scalar1=flag, scalar2=None, op0=ALU.mult),
                 reads=[BANK[2], CONSTB], writes=[b_uh])
            P.op("pool", R.dma_start(out=sps[0:60, :], in_=spool[l].rearrange("b r c -> (b r) c")), writes=[b_sps], stream=ST("sps"))
            for ci in range(4):
                P.op("pe", R.transpose(ps[:, 3, ci * 128:ci * 128 + 60], sps[0:60, ci * 128:(ci + 1) * 128], identf[0:60, 0:60]),
                     reads=[b_sps, CONSTB], writes=[BANK[3]])
            P.op("dve", R.tensor_copy(out=ushv[:, :, :, 0:15],
                                                in_=ps[:, 3, :].rearrange("p (c x) -> p c x", c=4)[:, :, 0:60].rearrange("p c (b r) -> p c b r", b=NS_TOK)),
                 reads=[BANK[3]], writes=[b_uh])
            wpool_slot = fetch(("wpool", l))
            wpl = aalloc(256, BF16, None)
            b_wpl = nbf("wpl", f2)
            P.op("dve", R.tensor_copy(out=wpl[:, :], in_=wbf[:, wpool_slot, 0:512]), reads=[WB[wpool_slot]], writes=[b_wpl])
            for ci in range(4):
                wn = 2 ** (ci + 1)
                wu_ = fetch(("in", l, "u", ci))

                def evu(c, c0, c1, n, bk, ci=ci):
                    npr = min(c1, NPT) - c0
                    P.op("act", R.activation(out=ue[:, 15 + c0:15 + c0 + npr], in_=ps[:, bk, 0:npr], func=AF.Copy), reads=[BANK[bk]], writes=[b_ue])
                    if c == 2:
                        P.op("act", R.activation(out=ushv[:, ci, :, 15], in_=ps[:, bk, npr:npr + NS_TOK], func=AF.Copy), reads=[BANK[bk]], writes=[b_uh])
                proj_fm(wu_, evu)
                P.op("dve", R.tensor_copy(out=ue[:, 0:15], in_=uh[:, ci, 0:15]), reads=[b_uh], writes=[b_ue])
                src, srcb = ue, b_ue
                dsts = [(ua, b_ua), (ub, b_ub)]
                step = 1
                di = 0
                while step < wn:
                    dst, dstb = dsts[di % 2]
                    lo = 2 * step - 1
                    P.op("dve", R.tensor_tensor(out=dst[:, lo:L], in0=src[:, lo:L], in1=src[:, lo - step:L - step], op=ALU.add),
                         reads=[srcb], writes=[dstb])
                    src, srcb = dst, dstb
                    di += 1
                    step *= 2
                oth, othb = (ub, b_ub) if src is ua else (ua, b_ua)
                P.op("dve", R.scalar_tensor_tensor(out=dd[:, 15:NPT], in0=src[:, 30:L], scalar=1.0 / wn, in1=ue[:, 30:L],
                                                                      op0=ALU.mult, op1=ALU.subtract),
                     reads=[srcb, b_ue], writes=[b_dd])
                P.op("dve", R.tensor_tensor(out=oth[:, 0:15], in0=src[:, 15:30],
                                                                               in1=invc.rearrange("p (g t) -> p g t", g=4)[:, ci, 0:15], op=ALU.mult),
                     reads=[srcb, CONSTB], writes=[othb])
                P.op("dve", R.tensor_tensor(out=dd[:, 0:15], in0=oth[:, 0:15], in1=ue[:, 15:30], op=ALU.subtract),
                     reads=[othb, b_ue], writes=[b_dd])
                P.op("dve", R.tensor_reduce(out=dds[:, 0:NS_TOK], in_=ushv[:, ci, :, 16 - wn:16], axis=AX.X, op=ALU.add),
                     reads=[b_uh], writes=[b_dds])
                P.op("dve", R.scalar_tensor_tensor(out=dd[:, NPT:T], in0=dds[:, 0:NS_TOK], scalar=1.0 / wn, in1=ushv[:, ci, :, 15],
                                                                           op0=ALU.mult, op1=ALU.subtract),
                     reads=[b_dds, b_uh], writes=[b_dd])
                psc = par[:, pl + PC_PS + ci:pl + PC_PS + ci + 1]
                for c, (c0, c1) in enumerate(CGS):
                    n = c1 - c0
                    bk = rot["gu"] % 4
                    rot["gu"] += 1
                    mm(ps[:, bk, 0:n], wpl[:, ci * 128:(ci + 1) * 128], dd[:, c0:c1], True, True, [b_wpl, b_dd], [BANK[bk]])
                    P.op("act", R.activation(out=mtg[:, ci, c0:c1], in_=ps[:, bk, 0:n], func=AF.Copy, scale=psc),
                         reads=[BANK[bk], PARB], writes=[b_mtg[ci][c]])
            wout_group(l, 0, mtg, b_mtg)

            if kmix <= 3:
                return
            P.op("dve", R.memset(scratch1[:, 3:4], 0.0), writes=[b_ue, b_ua, b_ub, b_dds, b_dd, b_sps, b_uh, b_tl, b_wpl])
            f3 = P.ops["dve"][-1]
            apos[0] = base_pos - 2 * T - HW
            tl_off = HW
            apos[0] = tl_off
            halb = aalloc(256, BF16, None)
            knb = aalloc(256, BF16, None)
            kn = aalloc(256, F32, None)
            vn = aalloc(256, F32, None)
            stat = aalloc(32, F32, [2, 16])
            sst = aalloc(96, F32, None)
            qm = aalloc(8 * NS_TOK * NS_TOK // 2, BF16, [8, NS_TOK, NS_TOK])
            qsall = aalloc(8 * NS_TOK // 2, BF16, [8, NS_TOK])
            spt = aalloc(8 * NS_TOK // 2, BF16, [8, NS_TOK])
            assert apos[0] <= 2 * HW, apos[0]
            apos[0] = base_pos
            kT = aalloc(4 * (128 + T) // 2, BF16, [4, 128 + T])
            vd = aalloc(10 * 512 // 2, BF16, [10, 512])
            kTs = aalloc(NS_TOK * 4 * 128 // 2, BF16, [NS_TOK, 4, 128])
            vds = aalloc(NS_TOK * 512 // 2, BF16, [NS_TOK, 512])
            qT = aalloc(T, BF16, [2, T])
            bh = aalloc(512, F32, [2, 256])
            sbt = aalloc(512, F32, [2, 256])
            pbt = aalloc(256, BF16, [2, 256])
            ptt = aalloc(256, BF16, [2, 256])
            bss = aalloc(8 * 128, F32, [8, 128])
            bhs = aalloc(256, F32, [2, 128])
            save_pos = apos[0]
            apos[0] = 0
            ssb = aalloc(8 * 128, F32, [8, 128])
            spb = aalloc(8 * 128 // 2, BF16, [8, 128])
            apos[0] = save_pos
            nb = lambda name: nbf(name, f3)
            b_kT = [nb(f"kT{g}") for g in range(4)]
            b_kh = nb("kTh")
            b_vd = [nb(f"vd{i}") for i in range(10)]
            b_halb, b_knb, b_kn, b_vn, b_kTs, b_vds = [nb(n) for n in ("halb", "knb", "kn", "vn", "kTs", "vds")]
            b_q = [nb("q0"), nb("q1")]
            b_bh = [nb("bh0"), nb("bh1")]
            b_bhs = [nb("bhs0"), nb("bhs1")]
            b_sb = [nb("sb0"), nb("sb1")]
            b_pb_ = [nb("pb0"), nb("pb1")]
            b_pt = [nb("pt0"), nb("pt1")]
            b_stt = [nb("st0"), nb("st1")]
            b_qm, b_qs, b_sst, b_bss, b_spt = [nb(n) for n in ("qm", "qs", "sst", "bss", "spt")]
            b_ssb = b_hal
            b_spb = b_hal
            for g in range(4):
                w = fetch(("in", l, "kd", g))

                def ev(c, c0, c1, n, bk, g=g):
                    P.op("act", R.activation(out=kT[:, g, 128 + c0:128 + c1], in_=ps[:, bk, 0:n], func=AF.Copy),
                         reads=[BANK[bk]], writes=[b_kT[g]])
                proj_fm(w, ev)
            for vu in range(2):
                w = fetch(("in", l, "v", vu))
                for tb in range(8):
                    bk = 4 + (vu * 8 + tb) % 4
                    proj_tm(w, tb * 128, 128, bk, 0)
                    for dup in range(2):
                        P.op("dve" if dup == 0 else "act",
                             (R.tensor_copy(out=vd[:, tb + 1, vu * 256:(vu + 1) * 256].rearrange("p (g r d) -> p g r d", g=2, r=2)[:, :, dup, :],
                                            in_=ps[:, bk, 0:128].rearrange("p (g d) -> p g d", g=2))) if dup == 0 else
                             (R.activation(out=vd[:, tb + 1, vu * 256:(vu + 1) * 256].rearrange("p (g r d) -> p g r d", g=2, r=2)[:, :, dup, :],
                                           in_=ps[:, bk, 0:128].rearrange("p (g d) -> p g d", g=2), func=AF.Copy)),
                             reads=[BANK[bk]], writes=[b_vd[tb + 1]])
            P.op("dve", R.tensor_copy(out=halb[:, :], in_=hal[:, TK:TK + 512]), reads=[b_hal], writes=[b_halb])
            for g in range(4):
                P.op("pe", R.transpose(psb[:, 0, g * 128:(g + 1) * 128], halb[:, g * 128:(g + 1) * 128], identb[:, :]),
                     reads=[b_halb, CONSTB], writes=[BANK[0]])
            P.op("dve", R.tensor_copy(out=kT[:, :, 0:128], in_=psb[:, 0, 0:512].rearrange("p (g t) -> p g t", g=4)),
                 reads=[BANK[0]], writes=[b_kh])
            for dup in range(2):
                P.op("dve", R.tensor_copy(out=vd[:, 0, :].rearrange("p (g r d) -> p g r d", g=4, r=2)[:, :, dup, :],
                                                             in_=hal[:, TV:TV + 256].rearrange("p (g d) -> p g d", g=4)),
                     reads=[b_hal], writes=[b_vd[0]])
            for b in range(NS_TOK):
                P.op("pool", R.dma_start(out=kn[:, :], in_=oks[l, b]), reads=[OKS], writes=[b_kn], stream=ST("kn"))
                P.op("pool", R.dma_start(out=vn[:, :], in_=ovs[l, b]), reads=[OKS], writes=[b_vn], stream=ST("vn"))
                for dup in range(2):
                    P.op("dve", R.tensor_copy(out=knb[:, :].rearrange("p (g r d) -> p g r d", g=4, r=2)[:, :, dup, :],
                                                                 in_=kn[:, :].rearrange("p (g d) -> p g d", g=4)),
                         reads=[b_kn], writes=[b_knb])
                    P.op("dve", R.tensor_copy(out=vds[:, b, :].rearrange("p (g r d) -> p g r d", g=4, r=2)[:, :, dup, :],
                                                                      in_=vn[:, :].rearrange("p (g d) -> p g d", g=4)),
                         reads=[b_vn], writes=[b_vds])
                for g in range(4):
                    P.op("pe", R.transpose(psb[:, 1, g * 128:(g + 1) * 128], knb[:, g * 128:(g + 1) * 128], identb[:, :]),
                         reads=[b_knb, CONSTB], writes=[BANK[1]])
                P.op("act", R.activation(out=kTs[:, b, :, :], in_=psb[:, 1, 0:512].rearrange("p (g t) -> p g t", g=4), func=AF.Copy),
                     reads=[BANK[1]], writes=[b_kTs])

            def sample_attn(half):
                h0 = 8 * half
                P.op("pool", R.dma_start(out=bss[0:NS_TOK, :, :], in_=bass.AP(gd.ap().tensor, 128 + h0 * GW, [[0, NS_TOK], [GW, 8], [1, 128]])),
                     reads=[BIASB], writes=[b_bss], stream=ST("bss"))
                for b in range(NS_TOK):
                    P.op("dve", R.tensor_copy(out=qm[:, h0 // 2:h0 // 2 + 4, b, b], in_=qsall[:, h0 // 2:h0 // 2 + 4, b]), reads=[b_qs], writes=[b_qm])
                for hi in range(8):
                    hh = h0 + hi
                    cq, ep, g = hh // 2, hh % 2, hh // 4
                    r0 = 64 * ep
                    bk = hi // 4
                    for b in range(NS_TOK):
                        mm(ps[0:NS_TOK, bk, (hi % 4) * 128:(hi % 4 + 1) * 128], qm[r0:r0 + 64, cq, b, :], kTs[r0:r0 + 64, b, g, :], b == 0, b == NS_TOK - 1,
                           [b_qm, b_kTs], [BANK[bk]])
                for bk in range(2):
                    P.op("dve", R.scalar_tensor_tensor(out=ssb[0:NS_TOK, 4 * bk:4 * bk + 4, :].rearrange("p a b -> p (a b)"), in0=ps[0:NS_TOK, bk, :], scalar=float(SCALE),
                                                                        in1=bss[0:NS_TOK, 4 * bk:4 * bk + 4, :].rearrange("p a b -> p (a b)"), op0=ALU.mult, op1=ALU.add),
                         reads=[BANK[bk], b_bss], writes=[b_ssb])
                s4 = lambda a: sst[0:NS_TOK, a * 8:(a + 1) * 8]
                sk = par[0:NS_TOK, pl + PC_SINK + h0:pl + PC_SINK + h0 + 8]
                P.op("dve", R.tensor_reduce(out=s4(0), in_=ssb[0:NS_TOK, :, :], axis=AX.X, op=ALU.max), reads=[b_ssb], writes=[b_sst])
                P.op("dve", R.tensor_tensor(out=s4(1), in0=s4(0), in1=sk, op=ALU.max), reads=[b_sst, PARB], writes=[b_sst])
                P.op("dve", R.tensor_scalar(out=s4(6), in0=s4(1), scalar1=-1.0, scalar2=None, op0=ALU.mult), reads=[b_sst], writes=[b_sst])
                for hi in range(8):
                    P.op("act", R.activation(out=ssb[0:NS_TOK, hi, :], in_=ssb[0:NS_TOK, hi, :], func=AF.Exp, bias=s4(6)[:, hi:hi + 1]),
                         reads=[b_ssb, b_sst], writes=[b_ssb])
                P.op("dve", R.tensor_reduce(out=s4(2), in_=ssb[0:NS_TOK, :, :], axis=AX.X, op=ALU.add), reads=[b_ssb], writes=[b_sst])
                P.op("dve", R.tensor_tensor(out=s4(3), in0=sk, in1=s4(1), op=ALU.subtract), reads=[b_sst, PARB], writes=[b_sst])
                P.op("act", R.activation(out=s4(3), in_=s4(3), func=AF.Exp), reads=[b_sst], writes=[b_sst])
                P.op("dve", R.tensor_tensor(out=s4(4), in0=s4(2), in1=s4(3), op=ALU.add), reads=[b_sst], writes=[b_sst])
                P.op("dve", R.reciprocal(out=s4(5), in_=s4(4)), reads=[b_sst], writes=[b_sst])
                for hi in range(8):
                    P.op("dve", R.tensor_scalar(out=spb[0:NS_TOK, hi, :], in0=ssb[0:NS_TOK, hi, :], scalar1=s4(5)[:, hi:hi + 1], scalar2=None, op0=ALU.mult),
                         reads=[b_ssb, b_sst], writes=[b_spb])
                ks = int(os.environ.get("KS", "9"))
                if ks <= 1:
                    return
                for hi in range(8):
                    P.op("pe", R.transpose(psb[:, 2, hi * NS_TOK:(hi + 1) * NS_TOK], spb[0:NS_TOK, hi, :], identb[0:NS_TOK, 0:NS_TOK]),
                         reads=[b_spb, CONSTB], writes=[BANK[2]])
                P.op("act", R.activation(out=spt[:, :, :], in_=psb[:, 2, 0:8 * NS_TOK].rearrange("p (h b) -> p h b", h=8), func=AF.Copy),
                     reads=[BANK[2]], writes=[b_spt])
                if ks <= 2:
                    return
                for b in range(NS_TOK):
                    for gi in range(2):
                        g = h0 // 4 + gi
                        mm(ps[:, 3, b * 8 + gi * 4:b * 8 + gi * 4 + 4], vds[:, b, g * 128:(g + 1) * 128], spt[:, gi * 4:gi * 4 + 4, b], True, True,
                           [b_vds, b_spt], [BANK[3]])
                if ks <= 3:
                    return
                for ep in range(2):
                    r0 = 64 * ep
                    src = ps[r0:r0 + 64, 3, 0:8 * NS_TOK].rearrange("p (b m e) -> p b m e", b=NS_TOK, m=4)[:, :, :, ep].rearrange("p b m -> p m b")
                    P.op("dve", R.tensor_copy(out=mtg[r0:r0 + 64, :, NPT:T], in_=src),
                         reads=[BANK[3]], writes=[b_mtg[j][2] for j in range(4)])

            ucnt = 0
            ka = int(os.environ.get("KA", "9"))
            for cq in range(8 if ka > 1 else 0):
                qi = cq % 2
                wq = fetch(("in", l, "q", cq))

                def evq(c, c0, c1, n, bk, qi=qi):
                    P.op("act", R.activation(out=qT[:, qi, c0:c1], in_=ps[:, bk, 0:n], func=AF.Copy), reads=[BANK[bk]], writes=[b_q[qi]])
                proj_fm(wq, evq)
                mj = cq % 4
                for ep in range(2):
                    hh = 2 * cq + ep
                    g = hh // 4
                    r0 = 64 * ep
                    bi = hh % 2
                    P.op("pool", R.dma_start(out=bh[:, bi, :], in_=bass.AP(wsk.ap().tensor, hh * 128 * GW + 127, [[GW - 1, 128], [1, 256]])),
                         reads=[BIASB], writes=[b_bh[bi]], stream=ST(f"bh{bi}"))
                    for blk in range(8):
                        u = ucnt % 2
                        ucnt += 1
                        bS = u
                        q0 = blk * 128
                        mm(ps[:, bS, 0:256], qT[r0:r0 + 64, qi, q0:q0 + 128], kT[r0:r0 + 64, g, q0:q0 + 256], True, True,
                           [b_q[qi], b_kT[g], b_kh], [BANK[bS]])
                        P.op("dve", R.scalar_tensor_tensor(out=sbt[:, u, :], in0=ps[:, bS, 0:256], scalar=float(SCALE), in1=bh[:, bi, :],
                                                                                       op0=ALU.mult, op1=ALU.add),
                             reads=[BANK[bS], b_bh[bi]], writes=[b_sb[u]])
                        if blk == 0:
                            P.op("dve", R.tensor_scalar(out=sbt[:, u, 0:128], in0=sbt[:, u, 0:128], scalar1=negflag, scalar2=None, op0=ALU.add),
                                 reads=[b_sb[u], CONSTB], writes=[b_sb[u]])
                        st_ = stat[:, u, :]
                        P.op("dve", R.tensor_reduce(out=st_[:, 0:1], in_=sbt[:, u, :], axis=AX.X, op=ALU.max), reads=[b_sb[u]], writes=[b_stt[u]])
                        P.op("dve", R.tensor_scalar(out=st_[:, 1:2], in0=st_[:, 0:1], scalar1=sink_ap(hh), scalar2=-1.0, op0=ALU.max, op1=ALU.mult),
                             reads=[b_stt[u], PARB], writes=[b_stt[u]])
                        P.op("act", R.activation(out=sbt[:, u, :], in_=sbt[:, u, :], func=AF.Exp, bias=st_[:, 1:2]),
                             reads=[b_sb[u], b_stt[u]], writes=[b_sb[u]])
                        P.op("dve", R.tensor_reduce(out=st_[:, 2:3], in_=sbt[:, u, :], axis=AX.X, op=ALU.add), reads=[b_sb[u]], writes=[b_stt[u]])
                        P.op("act", R.activation(out=st_[:, 3:4], in_=st_[:, 1:2], func=AF.Exp, bias=sink_ap(hh)),
                             reads=[b_stt[u], PARB], writes=[b_stt[u]])
                        P.op("dve", R.tensor_tensor(out=st_[:, 4:5], in0=st_[:, 2:3], in1=st_[:, 3:4], op=ALU.add), reads=[b_stt[u]], writes=[b_stt[u]])
                        P.op("dve", R.reciprocal(out=st_[:, 5:6], in_=st_[:, 4:5]), reads=[b_stt[u]], writes=[b_stt[u]])
                        P.op("dve", R.tensor_scalar(out=pbt[:, u, :], in0=sbt[:, u, :], scalar1=st_[:, 5:6], scalar2=None, op0=ALU.mult),
                             reads=[b_sb[u], b_stt[u]], writes=[b_pb_[u]])
                        bT = 2 + u
                        for hf in range(2):
                            P.op("pe", R.transpose(psb[:, bT, hf * 128:(hf + 1) * 128], pbt[:, u, hf * 128:(hf + 1) * 128], identb[:, :]),
                                 reads=[b_pb_[u], CONSTB], writes=[BANK[bT]])
                        P.op("act", R.activation(out=ptt[:, u, :], in_=psb[:, bT, 0:256], func=AF.Copy), reads=[BANK[bT]], writes=[b_pt[u]])
                        bO = 4 + (hh % 2) * 2 + (blk // 4)
                        oc = (blk % 4) * 128
                        for hf in range(2):
                            mm(ps[:, bO, oc:oc + 128], vd[:, blk + hf, g * 128:(g + 1) * 128], ptt[:, u, hf * 128:(hf + 1) * 128], hf == 0, hf == 1,
                               [b_vd[blk + hf], b_pt[u]], [BANK[bO]])
                        if blk % 4 == 3:
                            cc0 = (blk - 3) * 128
                            touch = [c for c, (a0, a1) in enumerate(CGS) if a0 < cc0 + 512 and a1 > cc0]
                            P.op("act", R.activation(out=mtg[r0:r0 + 64, mj, cc0:cc0 + 512], in_=ps[r0:r0 + 64, bO, :], func=AF.Copy),
                                 reads=[BANK[bO]], writes=[b_mtg[mj][c] for c in touch])
                for ep in range(2 if ka > 2 else 0):
                    hh = 2 * cq + ep
                    g = hh // 4
                    r0 = 64 * ep
                    bi = hh % 2
                    P.op("pool", R.dma_start(out=bhs[:, bi, :], in_=bass.AP(gd.ap().tensor, hh * GW + 128, [[0, 128], [1, 128]])),
                         reads=[BIASB], writes=[b_bhs[bi]], stream=ST(f"bhs{bi}"))
                    bOs = 4 + (hh % 2) * 2
                    for b in range(NS_TOK):
                        u = ucnt % 2
                        ucnt += 1
                        bS = u
                        mm(ps[:, bS, 0:128], qT[r0:r0 + 64, qi, T - 128:T], kTs[r0:r0 + 64, b, g, :], True, True,
                           [b_q[qi], b_kTs], [BANK[bS]])
                        P.op("dve", R.scalar_tensor_tensor(out=sbt[:, u, 0:128], in0=ps[:, bS, 0:128], scalar=float(SCALE), in1=bhs[:, bi, :],
                                                           op0=ALU.mult, op1=ALU.add),
                             reads=[BANK[bS], b_bhs[bi]], writes=[b_sb[u]])
                        st_ = stat[:, u, :]
                        P.op("dve", R.tensor_reduce(out=st_[:, 0:1], in_=sbt[:, u, 0:128], axis=AX.X, op=ALU.max), reads=[b_sb[u]], writes=[b_stt[u]])
                        P.op("dve", R.tensor_scalar(out=st_[:, 1:2], in0=st_[:, 0:1], scalar1=sink_ap(hh), scalar2=-1.0, op0=ALU.max, op1=ALU.mult),
                             reads=[b_stt[u], PARB], writes=[b_stt[u]])
                        P.op("act", R.activation(out=sbt[:, u, 0:128], in_=sbt[:, u, 0:128], func=AF.Exp, bias=st_[:, 1:2]),
                             reads=[b_sb[u], b_stt[u]], writes=[b_sb[u]])
                        P.op("dve", R.tensor_reduce(out=st_[:, 2:3], in_=sbt[:, u, 0:128], axis=AX.X, op=ALU.add), reads=[b_sb[u]], writes=[b_stt[u]])
                        P.op("act", R.activation(out=st_[:, 3:4], in_=st_[:, 1:2], func=AF.Exp, bias=sink_ap(hh)),
                             reads=[b_stt[u], PARB], writes=[b_stt[u]])
                        P.op("dve", R.tensor_tensor(out=st_[:, 4:5], in0=st_[:, 2:3], in1=st_[:, 3:4], op=ALU.add), reads=[b_stt[u]], writes=[b_stt[u]])
                        P.op("dve", R.reciprocal(out=st_[:, 5:6], in_=st_[:, 4:5]), reads=[b_stt[u]], writes=[b_stt[u]])
                        P.op("dve", R.tensor_scalar(out=pbt[:, u, 0:128], in0=sbt[:, u, 0:128], scalar1=st_[:, 5:6], scalar2=None, op0=ALU.mult),
                             reads=[b_sb[u], b_stt[u]], writes=[b_pb_[u]])
                        bT = 2 + u
                        P.op("pe", R.transpose(psb[:, bT, 0:128], pbt[:, u, 0:128], identb[:, :]), reads=[b_pb_[u], CONSTB], writes=[BANK[bT]])
                        P.op("act", R.activation(out=ptt[:, u, 0:128], in_=psb[:, bT, 0:128], func=AF.Copy), reads=[BANK[bT]], writes=[b_pt[u]])
                        mm(ps[:, bOs, b * 128:(b + 1) * 128], vds[:, b, g * 128:(g + 1) * 128], ptt[:, u, 0:128], True, True,
                           [b_vds, b_pt[u]], [BANK[bOs]])
                    for b in range(NS_TOK):
                        col = b * 128 + 124 + b
                        P.op("act", R.activation(out=mtg[r0:r0 + 64, mj, NPT + b:NPT + b + 1], in_=ps[r0:r0 + 64, bOs, col:col + 1], func=AF.Copy),
                             reads=[BANK[bOs]], writes=[b_mtg[mj][2]])
                if cq % 4 == 3:
                    wout_group(l, 2 + cq // 4, mtg, b_mtg)

        kstop = int(os.environ.get("KSTOP", "99"))
        for l in range(n_layers):
            stages = [lambda: ffn(l, 0), lambda: layer_norm(l, 0, True), lambda: mixer(l), lambda: layer_norm(l, 1, True),
                      lambda: ffn(l, 1), lambda: layer_norm(l, 2, False), lambda: ple(l, l == n_layers - 1)]
            for si, st_fn in enumerate(stages):
                if l * 7 + si + 1 > kstop:
                    break
                st_fn()
        for k in range(16):
            P.op("pool", R.dma_start(out=yT[k], in_=h[:, k, :]), reads=[Hb[k][0], Hb[k][1], Hb[k][2]], stream=st_out)

        for e_ in Prog.ENGS:
            cnt_ = 0
            for o in P.ops[e_]:
                if o.stream is None and o.signal:
                    cnt_ += 1
                    o.semval = cnt_

        def emit_one(name, eng):
            for o in P.ops[name]:
                for d in o.waits:
                    if d.stream is not None:
                        eng.wait_ge(d.stream.sem, d.stream.inc * (d.seq + 1))
                    else:
                        eng.wait_ge(P.sem[d.eng], d.semval)
                ins = getattr(eng, o.fn[0])(*o.fn[1], **o.fn[2])
                if o.stream is not None:
                    if o.stream.inc == 16:
                        ins.then_inc(o.stream.sem, 16)
                    else:
                        ins.then_inc(o.stream.sem)
                elif o.signal:
                    ins.then_inc(P.sem[name], 1)
            if name == "pool":
                for s in final_streams + list(_sn.values()):
                    if s.count > 0:
                        eng.wait_ge(s.sem, s.inc * s.count)

        assert len(specs) + LOOKA + NSTG + 2 <= NU_ALLOC, (len(specs), NU_ALLOC)
        block.sync(lambda e: emit_one("sp", e))
        block.scalar(lambda e: emit_one("act", e))
        block.vector(lambda e: emit_one("dve", e))
        block.gpsimd(lambda e: emit_one("pool", e))
        block.tensor(lambda e: emit_one("pe", e))
        print("ops:", {k: len(v) for k, v in P.ops.items()}, "units", len(specs), flush=True)
    return nc, specs


NU_TOTAL = DEPTH * 344 + LOOKA + NSTG + 2
_CACHE = {}


def _colunit(W, c0, ncols=128):
    K = W.shape[0]
    blk = W[:, c0:c0 + ncols].reshape(K // 128, 128, ncols).transpose(1, 0, 2).reshape(128, -1)
    return blk


def _make_unit(spec, w):
    kind = spec[0]
    out = np.zeros((128, UW), np.float32)
    if kind in ("gate", "up"):
        _, l, which, j = spec
        W = w[f"ffn{which + 1}_w_{kind}"][l]
        out[:] = _colunit(W, j * 128)
    elif kind == "down":
        _, l, which, j = spec
        out[:] = w[f"ffn{which + 1}_w_down"][l][j * 128:(j + 1) * 128, :]
    elif kind == "in":
        _, l, k2, idx = spec
        W = w["w_in"][l]
        if k2 == "u":
            out[:] = _colunit(W, idx * 128)
        elif k2 == "a":
            out[:] = _colunit(W, 512 + idx * 128)
        elif k2 == "g":
            out[:] = _colunit(W, 1024 + idx * 128)
        elif k2 == "q":
            out[:] = _colunit(W, 1536 + idx * 128)
        elif k2 == "kd":
            kk = _colunit(W, 2560 + idx * 64, 64).reshape(128, 16, 64)
            out[:] = np.concatenate([kk, kk], axis=2).reshape(128, UW)
        elif k2 == "v":
            out[:] = _colunit(W, 2816 + idx * 128)
    elif kind == "wout":
        _, l, r = spec
        out[:] = w["w_out"][l][r * 128:(r + 1) * 128, :]
    elif kind == "pw":
        _, l = spec
        out[:] = w["w_pw"][l].reshape(4, 128, 512).transpose(1, 0, 2).reshape(128, UW)
    elif kind == "wpool":
        _, l = spec
        out[:, 0:512] = w["w_pool"][l].transpose(1, 0, 2).reshape(128, 512)
    elif kind == "ple":
        _, l, i = spec
        out[:] = w["w_ple"][l][:, i * 1024:(i + 1) * 1024].reshape(2, 128, 1024).transpose(1, 0, 2).reshape(128, UW)
    elif kind == "pg":
        _, l, dq = spec
        out[:] = _colunit(w["w_ple_gate"][l], dq * 128)
    else:
        raise ValueError(kind)
    return out


def _fm(v, n):
    return np.asarray(v, np.float32).reshape(n, 128).T


def kernel(**inputs):
    w = {k: np.asarray(v) for k, v in inputs.items()}
    nl = int(os.environ.get("KDEPTH", DEPTH))
    nu_total = nl * 344 + LOOKA + NSTG + 2
    if "nc" not in _CACHE:
        _CACHE["nc"], _CACHE["specs"] = build(n_layers=nl, nunits_hint=nu_total)
    nc, specs = _CACHE["nc"], _CACHE["specs"]
    wstream = np.zeros((nu_total, 128, UW), np.float32)
    for i, sp_ in enumerate(specs):
        wstream[i] = _make_unit(sp_, w)
    params = np.zeros((128, DEPTH * PC_N), np.float32)
    for l in range(DEPTH):
        o = l * PC_N
        for i, nm in enumerate(["ln1_g", "ln1_b", "ln2_g", "ln2_b", "ln3_g", "ln3_b"]):
            params[:, o + PC_LN + 16 * i:o + PC_LN + 16 * (i + 1)] = _fm(w[nm][l], 16)
        params[:, o + PC_PS:o + PC_PS + 4] = _fm(w["pool_scale"][l], 4)
        params[:, o + PC_BDW:o + PC_BDW + 4] = _fm(w["b_dw"][l], 4)
        params[:, o + PC_CLG:o + PC_CLG + 4] = _fm(w["conv_ln_g"][l], 4)
        params[:, o + PC_CLB:o + PC_CLB + 4] = _fm(w["conv_ln_b"][l], 4)
        params[:, o + PC_WDW:o + PC_WDW + 124] = w["w_dw"][l].T.reshape(4, 128, 31).transpose(1, 0, 2).reshape(128, 124)
        params[:, o + PC_SINK:o + PC_SINK + 16] = np.broadcast_to(w["sinks"][l][None, :], (128, 16))
    relaug = np.concatenate([w["rel_bias"].astype(np.float32), np.ones((1, 16), np.float32)], 0)
    ohaug = np.zeros((33, GW), np.float32)
    bt = _bucket_table()
    for m in range(GW):
        dist = 255 - m
        if 0 <= dist < 128:
            ohaug[bt[dist], m] = 1.0
        else:
            ohaug[32, m] = -1e30
    in_maps = []
    for c in range(8):
        s, half = c // 2, c % 2
        xs = np.concatenate([w["x_prompt"][s, half * NPT:(half + 1) * NPT, :], w["x_sample"][4 * c:4 * c + 4, 0, :]], 0)
        xT = np.ascontiguousarray(xs.T).reshape(16, 128, T)
        pTs = []
        for l in range(DEPTH):
            pp = np.concatenate([w["p_prompt"][l, s, half * NPT:(half + 1) * NPT, :], w["p_sample"][l, 4 * c:4 * c + 4, 0, :]], 0)
            pTs.append(np.ascontiguousarray(pp.T).reshape(2, 128, T))
        cst = np.zeros((128, 208), np.float32)
        cst[:, 0:128] = np.eye(128, dtype=np.float32)
        cst[:, 128] = float(half)
        cst[:, 129] = 0.0 if half == 1 else -1e30
        for g in range(4):
            wn = 2 ** (g + 1)
            for t in range(16):
                cst[:, 144 + g * 16 + t] = 1.0 / (min(t + 1, wn) if half == 0 else wn)
        in_maps.append({
            "xT": xT.astype(np.float32), "pT": np.stack(pTs).astype(np.float32), "wstream": wstream, "params": params,
            "ck": np.ascontiguousarray(w["cache_k"][:, 4 * c:4 * c + 4].reshape(DEPTH, 4, 128, 256)),
            "cv": np.ascontiguousarray(w["cache_v"][:, 4 * c:4 * c + 4].reshape(DEPTH, 4, 128, 256)),
            "spool": np.ascontiguousarray(w["state_pool"][:, 4 * c:4 * c + 4]),
            "sconv": np.ascontiguousarray(w["state_conv"][:, 4 * c:4 * c + 4]),
            "relaug": relaug, "ohaug": ohaug, "cst": cst,
        })
    res = run_bass_kernel_spmd(nc, in_maps, core_ids=list(range(8)))
    R = res.results
    y_prompt = np.zeros((4, 2048, D), np.float32)
    y_sample = np.zeros((32, 1, D), np.float32)
    nkp = np.zeros((DEPTH, 4, 128, 4, 64), np.float32)
    nvp = np.zeros_like(nkp)
    npp = np.zeros((DEPTH, 4, 15, 512), np.float32)
    ncp = np.zeros((DEPTH, 4, 30, 512), np.float32)
    nks = np.zeros((DEPTH, 32, 128, 4, 64), np.float32)
    nvs = np.zeros_like(nks)
    nps = np.zeros((DEPTH, 32, 15, 512), np.float32)
    ncs = np.zeros((DEPTH, 32, 30, 512), np.float32)
    for c in range(8):
        s, half = c // 2, c % 2
        r = R[c]
        yt = np.asarray(r["yT"]).reshape(D, T)
        y_prompt[s, half * NPT:(half + 1) * NPT, :] = yt[:, :NPT].T
        y_sample[4 * c:4 * c + 4, 0, :] = yt[:, NPT:].T
        if half == 1:
            nkp[:, s] = np.asarray(r["okp"]).reshape(DEPTH, 128, 4, 64)
            nvp[:, s] = np.asarray(r["ovp"]).reshape(DEPTH, 128, 4, 64)
            npp[:, s] = np.asarray(r["opp"])
            ncp[:, s] = np.asarray(r["ocp"])
        nks[:, 4 * c:4 * c + 4] = np.asarray(r["oks"]).reshape(DEPTH, 4, 128, 4, 64)
        nvs[:, 4 * c:4 * c + 4] = np.asarray(r["ovs"]).reshape(DEPTH, 4, 128, 4, 64)
        nps[:, 4 * c:4 * c + 4] = np.asarray(r["ops"])
        ncs[:, 4 * c:4 * c + 4] = np.asarray(r["ocs"])
    return (y_prompt, y_sample, nkp, nvp, npp, ncp, nks, nvs, nps, ncs)
```

```python
import math
import os
import numpy as np
import concourse.bass as bass
import concourse.mybir as mybir
from concourse.bass_utils import run_bass_kernel_spmd

F32 = mybir.dt.float32
BF16 = mybir.dt.bfloat16
AF = mybir.ActivationFunctionType
ALU = mybir.AluOpType
AX = mybir.AxisListType

D = 2048
DEPTH = 4
NPT = 1024
NS_TOK = 4
T = NPT + NS_TOK
CGS = [(0, 343), (343, 686), (686, 1028)]
DFF = 5632
NFF = DFF // 128
G = 4
NGRP = NFF // G
ALPHA = (2 * DEPTH) ** 0.25
LN_EPS = 1e-5
SCALE = 64 ** -0.5
UW = 2048
NSTG = 2
NWB = 6
LOOKA = 3
ARENA_W = 16680
GW = 384

PC_LN = 0
PC_PS = 96
PC_BDW = 100
PC_CLG = 104
PC_CLB = 108
PC_WDW = 112
PC_SINK = 236
PC_N = 252


class _Rec:
    def __getattr__(self, name):
        def f(*a, **k):
            return (name, a, k)
        return f


R = _Rec()


class Buf:
    __slots__ = ("w", "rs", "name")

    def __init__(self, name=""):
        self.w = None
        self.rs = []
        self.name = name


class Stream:
    def __init__(self, sem, inc=16):
        self.sem = sem
        self.inc = inc
        self.count = 0


class Op:
    __slots__ = ("eng", "fn", "waits", "signal", "idx", "stream", "seq", "semval")


class Prog:
    ENGS = ["pe", "act", "dve", "pool", "sp"]

    def __init__(self):
        self.ops = {e: [] for e in self.ENGS}
        self.seen = {e: {} for e in self.ENGS}
        self.sem = {}

    def op(self, eng, fn, reads=(), writes=(), stream=None):
        o = Op()
        o.eng = eng
        o.fn = fn
        o.waits = []
        o.signal = False
        o.idx = len(self.ops[eng])
        o.stream = stream
        o.seq = 0
        o.semval = 0
        if stream is not None:
            o.seq = stream.count
            stream.count += 1
        deps = []
        for b in reads:
            if b.w is not None:
                deps.append(b.w)
        for b in writes:
            if b.w is not None:
                deps.append(b.w)
            deps.extend(b.rs)
        seen = self.seen[eng]
        best = {}
        for d in deps:
            if d.stream is None:
                if d.eng == eng and eng == "pe":
                    continue
                key = d.eng
                val = d.idx
            else:
                key = d.stream
                val = d.seq
            if seen.get(key, -1) >= val:
                continue
            if key not in best or best[key][0] < val:
                best[key] = (val, d)
        for key, (val, d) in best.items():
            seen[key] = val
            d.signal = True
            o.waits.append(d)
        for b in writes:
            b.w = o
            b.rs = []
        for b in reads:
            if b.w is not o:
                b.rs.append(o)
        self.ops[eng].append(o)
        return o

    def emit(self, engines, final_streams):
        for e in self.ENGS:
            cnt = 0
            for o in self.ops[e]:
                if o.stream is None and o.signal:
                    cnt += 1
                    o.semval = cnt
        for e in self.ENGS:
            eng = engines[e]
            for o in self.ops[e]:
                for d in o.waits:
                    if d.stream is not None:
                        eng.wait_ge(d.stream.sem, d.stream.inc * (d.seq + 1))
                    else:
                        eng.wait_ge(self.sem[d.eng], d.semval)
                ins = getattr(eng, o.fn[0])(*o.fn[1], **o.fn[2])
                if o.stream is not None:
                    if o.stream.inc == 16:
                        ins.then_inc(o.stream.sem, 16)
                    else:
                        ins.then_inc(o.stream.sem)
                elif o.signal:
                    ins.then_inc(self.sem[e], 1)
            if e == "sp":
                for s in final_streams:
                    if s.count > 0:
                        eng.wait_ge(s.sem, s.inc * s.count)


def _bucket_table():
    n = np.arange(128)
    nf = np.maximum(n, 1).astype(np.float32)
    logb = 16 + (np.log(nf / np.float32(16)) / np.float32(math.log(8.0)) * np.float32(16)).astype(np.int32)
    return np.where(n < 16, n, np.minimum(logb, 31))


def build(n_layers=DEPTH, nunits_hint=None, stop_stage=None):
    nc = bass.Bass("TRN2", target_bir_lowering=False)
    P = Prog()
    specs = []

    def din(name, shape, dt=F32):
        return nc.dram_tensor(name, list(shape), dt, kind="ExternalInput").ap()

    def dout(name, shape, dt=F32):
        return nc.dram_tensor(name, list(shape), dt, kind="ExternalOutput").ap()

    NU_ALLOC = nunits_hint if nunits_hint is not None else 1
    xT = din("xT", [16, 128, T])
    pT = din("pT", [DEPTH, 2, 128, T])
    wstream = din("wstream", [NU_ALLOC, 128, UW])
    params = din("params", [128, DEPTH * PC_N])
    ck = din("ck", [DEPTH, NS_TOK, 128, 256])
    cv = din("cv", [DEPTH, NS_TOK, 128, 256])
    spool = din("spool", [DEPTH, NS_TOK, 15, 512])
    sconv = din("sconv", [DEPTH, NS_TOK, 30, 512])
    relaug = din("relaug", [33, 16])
    ohaug = din("ohaug", [33, GW])
    cst = din("cst", [128, 128 + 16 + 64])
    yT = dout("yT", [16, 128, T])
    okp = dout("okp", [DEPTH, 128, 256])
    ovp = dout("ovp", [DEPTH, 128, 256])
    opp = dout("opp", [DEPTH, 15, 512])
    ocp = dout("ocp", [DEPTH, 30, 512])
    oks = dout("oks", [DEPTH, NS_TOK, 128, 256])
    ovs = dout("ovs", [DEPTH, NS_TOK, 128, 256])
    ops_ = dout("ops", [DEPTH, NS_TOK, 15, 512])
    ocs = dout("ocs", [DEPTH, NS_TOK, 30, 512])
    gd = nc.dram_tensor("gd", [16, GW], F32)
    wsk = nc.dram_tensor("wsk", [16, 128, GW], F32)
    HW = 512 + 256 + 512 + 512
    send = [nc.dram_tensor(f"send{l}", [128, HW], F32) for l in range(n_layers)]
    recv = [nc.dram_tensor(f"recv{l}", [256, HW], F32) for l in range(n_layers)]

    import contextlib
    with contextlib.ExitStack() as es:
        def sb(name, shape, dt):
            return es.enter_context(nc.sbuf_tensor(name, list(shape), dt))

        h = sb("h", [128, 16, T], F32)
        hb = sb("hb", [128, 16, T], BF16)
        stg = sb("stg", [128, NSTG, UW], F32)
        wbf = sb("wbf", [128, NWB, UW], BF16)
        par = sb("par", [128, DEPTH * PC_N], F32)
        par2 = sb("par2", [128, DEPTH * 64], F32)
        cs = sb("cs", [128, 128 + 16 + 64], F32)
        identb = sb("identb", [128, 128], BF16)
        onesb = sb("onesb", [128, 128], BF16)
        arena = sb("arena", [128, ARENA_W], F32)
        ps = es.enter_context(nc.psum_tensor("ps", [128, 8, 512], F32))
        psb = ps.bitcast(BF16) if hasattr(ps, "bitcast") else ps[:, :, :].bitcast(BF16)

        def sem(name):
            return es.enter_context(nc.semaphore(name))

        for e in Prog.ENGS:
            P.sem[e] = sem("s_" + e)
        st_init = Stream(sem("st_init"))
        st_stg = [Stream(sem(f"st_stg{i}")) for i in range(NSTG)]
        st_out = Stream(sem("st_out"))
        st_misc = [Stream(sem(f"st_misc{i}")) for i in range(12)]
        st_cc = Stream(sem("st_cc"), inc=1)
        st_x = [Stream(sem(f"st_x{i}")) for i in range(16)]
        _sn = {}

        def ST(name):
            if name not in _sn:
                _sn[name] = Stream(sem("sx_" + name))
            return _sn[name]
        st_out2 = Stream(sem("st_out2"))
        final_streams = [st_out, st_out2]
        block = es.enter_context(nc.Block())

        identf = cs[:, 0:128]
        flag = cs[:, 128:129]
        negflag = cs[:, 129:130]
        invc = cs[:, 144:208]

        Hb = [[Buf(f"h{k}_{c}") for c in range(3)] for k in range(16)]
        HBb = [[Buf(f"hb{k}_{c}") for c in range(3)] for k in range(16)]
        STG = [Buf(f"stg{i}") for i in range(NSTG)]
        WB = [Buf(f"wb{i}") for i in range(NWB)]
        BANK = [Buf(f"bank{i}") for i in range(8)]
        PARB = Buf("par")
        CONSTB = Buf("const")
        arena_bufs = []
        fence = [None]

        def abuf(name):
            b = Buf(name)
            b.w = fence[0]
            arena_bufs.append(b)
            return b

        scratch1 = sb("scr1", [128, 4], F32)

        def arena_reset():
            bl = list(arena_bufs)
            o = P.op("dve", R.memset(scratch1[:, 0:1], 0.0), writes=bl)
            fence[0] = o
            arena_bufs.clear()
            apos[0] = 0

        apos = [0]

        def aalloc(nwords, dt, shape):
            off = apos[0]
            apos[0] += nwords
            assert apos[0] <= ARENA_W, f"arena overflow {apos[0]}"
            v = arena[:, off:off + nwords]
            if dt is BF16:
                v = v.bitcast(BF16)
            if shape is None:
                return v
            if len(shape) == 2:
                return v.rearrange("p (a b) -> p a b", a=shape[0])
            if len(shape) == 3:
                return v.rearrange("p (a b c) -> p a b c", a=shape[0], b=shape[1])
            return v

        issued = {"dma": 0, "cast": 0}

        def issue_dma(i):
            s = i % NSTG
            P.op("sp", R.dma_start(out=stg[:, s, :], in_=wstream[min(i, NU_ALLOC - 1)]),
                 writes=[STG[s]], stream=st_stg[s])

        def issue_cast(i):
            s = i % NSTG
            w = i % NWB
            if i % 2 == 0:
                P.op("act", R.activation(out=wbf[:, w, :], in_=stg[:, s, :], func=AF.Copy),
                     reads=[STG[s]], writes=[WB[w]])
            else:
                P.op("pool", R.tensor_copy(out=wbf[:, w, :], in_=stg[:, s, :]),
                     reads=[STG[s]], writes=[WB[w]])

        def fetch(spec):
            n = len(specs)
            specs.append(spec)
            while issued["cast"] <= n + LOOKA - 1:
                i = issued["cast"]
                while issued["dma"] <= i:
                    issue_dma(issued["dma"])
                    issued["dma"] += 1
                issue_cast(i)
                issued["cast"] += 1
            return n % NWB

        rot = {"gu": 0, "dn": 0}

        def mm(out, lhsT, rhs, start, stop, reads, writes):
            return P.op("pe", R.matmul(out, lhsT=lhsT, rhs=rhs, start=start, stop=stop),
                        reads=reads, writes=writes)

        P.op("pool", R.dma_start(out=par[:, :], in_=params), writes=[PARB], stream=st_init)
        P.op("pool", R.dma_start(out=cs[:, :], in_=cst), writes=[CONSTB], stream=st_init)
        for k in range(16):
            P.op("pool", R.dma_start(out=h[:, k, :], in_=xT[k]),
                 writes=[Hb[k][0], Hb[k][1], Hb[k][2]], stream=st_x[k])
        P.op("dve", R.tensor_copy(out=identb[:, :], in_=identf), reads=[CONSTB], writes=[CONSTB])
        P.op("dve", R.memset(onesb[:, :], 1.0), writes=[CONSTB])
        for l in range(n_layers):
            src = par[:, l * PC_N:l * PC_N + 64]
            P.op("dve", R.tensor_scalar(out=par2[:, l * 64:(l + 1) * 64], in0=src,
                                                                  scalar1=float(ALPHA), scalar2=None, op0=ALU.mult),
                 reads=[PARB], writes=[PARB])
        for k in range(16):
            for c, (c0, c1) in enumerate(CGS):
                P.op("act", R.activation(out=hb[:, k, c0:c1], in_=h[:, k, c0:c1], func=AF.Copy),
                     reads=[Hb[k][c]], writes=[HBb[k][c]])
                P.op("dve", R.tensor_scalar(out=h[:, k, c0:c1], in0=h[:, k, c0:c1],
                                                                          scalar1=float(ALPHA), scalar2=None, op0=ALU.mult),
                     reads=[Hb[k][c]], writes=[Hb[k][c]])

        BIASB = Buf("biasdram")
        arena_reset()
        ra = aalloc(16, F32, None)
        oh = aalloc(GW, F32, None)
        gsb = aalloc(GW, F32, None)
        b_ra = abuf("ra")
        b_gsb = abuf("gsb")
        P.op("pool", R.dma_start(out=ra[0:33, :], in_=relaug), writes=[b_ra], stream=ST("ra"))
        P.op("pool", R.dma_start(out=oh[0:33, :], in_=ohaug), writes=[b_ra], stream=ST("oh"))
        P.op("pe", R.matmul(ps[0:16, 0, 0:GW], lhsT=ra[0:33, :], rhs=oh[0:33, :], start=True, stop=True),
             reads=[b_ra], writes=[BANK[0]])
        P.op("dve", R.tensor_copy(out=gsb[0:16, :], in_=ps[0:16, 0, 0:GW]), reads=[BANK[0]], writes=[b_gsb])
        P.op("pool", R.dma_start(out=gd[:, :], in_=gsb[0:16, :]), reads=[b_gsb], writes=[BIASB], stream=ST("gd"))
        P.op("pool", R.dma_start(out=wsk[:, :, :], in_=bass.AP(gd.ap().tensor, 0, [[GW, 16], [0, 128], [1, GW]])),
             reads=[BIASB], writes=[BIASB], stream=ST("wsk"))

        def layer_norm(l, which, scaled):
            arena_reset()
            pb = l * PC_N + PC_LN + which * 32
            g_ap = lambda k: par[:, pb + k:pb + k + 1]
            b_ap = lambda k: par[:, pb + 16 + k:pb + 17 + k]
            if scaled:
                q = l * 64 + which * 32
                gs_ap = lambda k: par2[:, q + k:q + k + 1]
                bs_ap = lambda k: par2[:, q + 16 + k:q + 17 + k]
            else:
                gs_ap, bs_ap = g_ap, b_ap
            rb = aalloc(344, BF16, [2, 344])
            rsq = aalloc(344, BF16, [2, 344])
            mean = aalloc(344, F32, None)
            var = aalloc(344, F32, None)
            rstd = aalloc(344, F32, None)
            t1 = aalloc(688, F32, [2, 344])
            t2 = aalloc(688, F32, [2, 344])
            b_rb = [abuf("rb0"), abuf("rb1")]
            b_rsq = [abuf("rsq0"), abuf("rsq1")]
            b_st = abuf("stats")
            b_t1 = [abuf("t10"), abuf("t11")]
            b_t2 = [abuf("t20"), abuf("t21")]
            for c, (c0, c1) in enumerate(CGS):
                n = c1 - c0
                bk1, bk2 = 4 + 2 * (c % 2), 5 + 2 * (c % 2)
                for k in range(16):
                    i = k % 2
                    P.op("dve", R.tensor_copy(out=rb[:, i, 0:n], in_=h[:, k, c0:c1]),
                         reads=[Hb[k][c]], writes=[b_rb[i]])
                    P.op("act", R.activation(out=rsq[:, i, 0:n], in_=h[:, k, c0:c1], func=AF.Square),
                         reads=[Hb[k][c]], writes=[b_rsq[i]])
                    mm(ps[:, bk1, 0:n], onesb[:, :], rb[:, i, 0:n], k == 0, k == 15, [b_rb[i], CONSTB], [BANK[bk1]])
                    mm(ps[:, bk2, 0:n], onesb[:, :], rsq[:, i, 0:n], k == 0, k == 15, [b_rsq[i], CONSTB], [BANK[bk2]])
                P.op("dve", R.tensor_scalar(out=mean[:, 0:n], in0=ps[:, bk1, 0:n], scalar1=1.0 / D, scalar2=None, op0=ALU.mult),
                     reads=[BANK[bk1]], writes=[b_st])
                P.op("dve", R.tensor_tensor(out=var[:, 0:n], in0=mean[:, 0:n], in1=mean[:, 0:n], op=ALU.mult),
                     reads=[b_st], writes=[b_st])
                P.op("dve", R.scalar_tensor_tensor(out=var[:, 0:n], in0=ps[:, bk2, 0:n], scalar=1.0 / D, in1=var[:, 0:n],
                                                             op0=ALU.mult, op1=ALU.subtract),
                     reads=[BANK[bk2], b_st], writes=[b_st])
                P.op("dve", R.tensor_scalar(out=var[:, 0:n], in0=var[:, 0:n], scalar1=float(LN_EPS), scalar2=None, op0=ALU.add),
                     reads=[b_st], writes=[b_st])
                P.op("act", R.activation(out=var[:, 0:n], in_=var[:, 0:n], func=AF.Sqrt), reads=[b_st], writes=[b_st])
                P.op("dve", R.reciprocal(out=rstd[:, 0:n], in_=var[:, 0:n]), reads=[b_st], writes=[b_st])
                for k in range(16):
                    i = k % 2
                    P.op("pool", R.tensor_tensor(out=t1[:, i, 0:n], in0=h[:, k, c0:c1], in1=mean[:, 0:n], op=ALU.subtract),
                         reads=[Hb[k][c], b_st], writes=[b_t1[i]])
                    P.op("dve", R.tensor_tensor(out=t2[:, i, 0:n], in0=t1[:, i, 0:n], in1=rstd[:, 0:n], op=ALU.mult),
                         reads=[b_t1[i], b_st], writes=[b_t2[i]])
                    P.op("act", R.activation(out=h[:, k, c0:c1], in_=t2[:, i, 0:n], func=AF.Identity,
                                                                 scale=gs_ap(k), bias=bs_ap(k)),
                         reads=[b_t2[i], PARB], writes=[Hb[k][c]])
                    P.op("dve", R.tensor_scalar(out=hb[:, k, c0:c1], in0=t2[:, i, 0:n], scalar1=g_ap(k), scalar2=b_ap(k),
                                                op0=ALU.mult, op1=ALU.add),
                         reads=[b_t2[i], PARB], writes=[HBb[k][c]])

        def ffn(l, which):
            arena_reset()
            act = aalloc(2 * G * 514, BF16, [2 * G, T])
            sg = aalloc(688, F32, [2, 344])
            b_act = [[[abuf(f"act{p}_{j}_{c}") for c in range(3)] for j in range(G)] for p in range(2)]
            b_sg = [abuf("sg0"), abuf("sg1")]
            cnt = {"gu": 0, "dn": 0, "sg": 0}

            def gateup(grp):
                p = grp % 2
                for jj in range(G):
                    j = grp * G + jj
                    wg = fetch(("gate", l, which, j))
                    wu = fetch(("up", l, which, j))
                    for c, (c0, c1) in enumerate(CGS):
                        n = c1 - c0
                        pair = cnt["gu"] % 2
                        cnt["gu"] += 1
                        bg, bu = 2 * pair, 2 * pair + 1
                        for k in range(16):
                            mm(ps[:, bg, 0:n], wbf[:, wg, k * 128:(k + 1) * 128], hb[:, k, c0:c1], k == 0, k == 15,
                               [WB[wg], HBb[k][c]], [BANK[bg]])
                        for k in range(16):
                            mm(ps[:, bu, 0:n], wbf[:, wu, k * 128:(k + 1) * 128], hb[:, k, c0:c1], k == 0, k == 15,
                               [WB[wu], HBb[k][c]], [BANK[bu]])
                        si = cnt["sg"] % 2
                        cnt["sg"] += 1
                        P.op("act", R.activation(out=sg[:, si, 0:n], in_=ps[:, bg, 0:n], func=AF.Silu),
                             reads=[BANK[bg]], writes=[b_sg[si]])
                        P.op("dve", R.tensor_tensor(out=act[:, p * G + jj, c0:c1], in0=sg[:, si, 0:n], in1=ps[:, bu, 0:n], op=ALU.mult),
                             reads=[b_sg[si], BANK[bu]], writes=[b_act[p][jj][c]])

            def down(grp):
                p = grp % 2
                wd = [fetch(("down", l, which, grp * G + jj)) for jj in range(G)]
                for dq in range(16):
                    for c, (c0, c1) in enumerate(CGS):
                        n = c1 - c0
                        bk = 4 + cnt["dn"] % 4
                        cnt["dn"] += 1
                        for jj in range(G):
                            mm(ps[:, bk, 0:n], wbf[:, wd[jj], dq * 128:(dq + 1) * 128], act[:, p * G + jj, c0:c1],
                               jj == 0, jj == G - 1, [WB[wd[jj]], b_act[p][jj][c]], [BANK[bk]])
                        P.op("dve", R.scalar_tensor_tensor(out=h[:, dq, c0:c1], in0=ps[:, bk, 0:n], scalar=0.5, in1=h[:, dq, c0:c1],
                                                    op0=ALU.mult, op1=ALU.add),
                             reads=[BANK[bk], Hb[dq][c]], writes=[Hb[dq][c]])

            gateup(0)
            for grp in range(NGRP):
                if grp + 1 < NGRP:
                    gateup(grp + 1)
                down(grp)

        def ple(l, last):
            arena_reset()
            pf = aalloc(2 * T, F32, [2, T])
            pbf = aalloc(T, BF16, [2, T])
            sgt = aalloc(688, F32, [2, 344])
            b_pf = abuf("pf")
            b_pb = abuf("pb")
            b_s = [abuf("s0"), abuf("s1")]
            for kk in range(2):
                P.op("pool", R.dma_start(out=pf[:, kk, :], in_=pT[l, kk]), writes=[b_pf], stream=ST(f"pT{kk}"))
            P.op("dve", R.tensor_copy(out=pbf[:, :, :], in_=pf[:, :, :]), reads=[b_pf], writes=[b_pb])
            wple = aalloc(2 * UW // 2, BF16, [2, UW])
            b_wple = abuf("wple")
            for i in range(2):
                wsl = fetch(("ple", l, i))
                P.op("dve", R.tensor_copy(out=wple[:, i, :], in_=wbf[:, wsl, :]), reads=[WB[wsl]], writes=[b_wple])
            cnt = 0
            for dq in range(16):
                wg = fetch(("pg", l, dq))
                for c, (c0, c1) in enumerate(CGS):
                    n = c1 - c0
                    pair = cnt % 2
                    cnt += 1
                    bg, bu = 2 * pair, 2 * pair + 1
                    for k in range(16):
                        mm(ps[:, bg, 0:n], wbf[:, wg, k * 128:(k + 1) * 128], hb[:, k, c0:c1], k == 0, k == 15,
                           [WB[wg], HBb[k][c]], [BANK[bg]])
                    wpi = dq // 8
                    off = (dq % 8) * 128
                    for kk in range(2):
                        mm(ps[:, bu, 0:n], wple[:, wpi, kk * 1024 + off:kk * 1024 + off + 128], pbf[:, kk, c0:c1], kk == 0, kk == 1,
                           [b_wple, b_pb], [BANK[bu]])
                    si = pair
                    P.op("act", R.activation(out=sgt[:, si, 0:n], in_=ps[:, bg, 0:n], func=AF.Sigmoid),
                         reads=[BANK[bg]], writes=[b_s[si]])
                    P.op("dve", R.tensor_tensor(out=sgt[:, si, 0:n], in0=sgt[:, si, 0:n], in1=ps[:, bu, 0:n], op=ALU.mult),
                         reads=[b_s[si], BANK[bu]], writes=[b_s[si]])
                    P.op("pool", R.tensor_tensor(out=h[:, dq, c0:c1], in0=h[:, dq, c0:c1], in1=sgt[:, si, 0:n], op=ALU.add),
                         reads=[b_s[si], Hb[dq][c]], writes=[Hb[dq][c]])
            for k in range(16):
                for c, (c0, c1) in enumerate(CGS):
                    if last:
                        continue
                    P.op("act", R.activation(out=hb[:, k, c0:c1], in_=h[:, k, c0:c1], func=AF.Copy),
                         reads=[Hb[k][c]], writes=[HBb[k][c]])
                    P.op("dve", R.tensor_scalar(out=h[:, k, c0:c1], in0=h[:, k, c0:c1],
                                                                              scalar1=float(ALPHA), scalar2=None, op0=ALU.mult),
                         reads=[Hb[k][c]], writes=[Hb[k][c]])

        def proj_fm(wslot, evac):
            for c, (c0, c1) in enumerate(CGS):
                n = c1 - c0
                bk = rot["gu"] % 4
                rot["gu"] += 1
                for k in range(16):
                    mm(ps[:, bk, 0:n], wbf[:, wslot, k * 128:(k + 1) * 128], hb[:, k, c0:c1], k == 0, k == 15,
                       [WB[wslot], HBb[k][c]], [BANK[bk]])
                evac(c, c0, c1, n, bk)

        def proj_tm(wslot, t0, nt, bk, col0):
            cs_ = [c for c, (a0, a1) in enumerate(CGS) if a0 < t0 + nt and a1 > t0]
            for k in range(16):
                mm(ps[0:nt, bk, col0:col0 + 128], hb[:, k, t0:t0 + nt], wbf[:, wslot, k * 128:(k + 1) * 128], k == 0, k == 15,
                   [WB[wslot]] + [HBb[k][c] for c in cs_], [BANK[bk]])

        def wout_group(l, grp, mtg, b_mtg):
            wo = [fetch(("wout", l, grp * 4 + jj)) for jj in range(4)]
            for dq in range(16):
                for c, (c0, c1) in enumerate(CGS):
                    n = c1 - c0
                    bk = 4 + rot["dn"] % 4
                    rot["dn"] += 1
                    for jj in range(4):
                        mm(ps[:, bk, 0:n], wbf[:, wo[jj], dq * 128:(dq + 1) * 128], mtg[:, jj, c0:c1], jj == 0, jj == 3,
                           [WB[wo[jj]], b_mtg[jj][c]], [BANK[bk]])
                    P.op("dve", R.tensor_tensor(out=h[:, dq, c0:c1], in0=ps[:, bk, 0:n], in1=h[:, dq, c0:c1], op=ALU.add),
                         reads=[BANK[bk], Hb[dq][c]], writes=[Hb[dq][c]])

        def mixer(l):
            pl = l * PC_N
            TK, TV, TU, TC = 0, 512, 768, 1280
            t0p, t0s = NPT - 128, NPT
            sink_ap = lambda hh: par[:, pl + PC_SINK + hh:pl + PC_SINK + hh + 1]
            arena_reset()
            hal = aalloc(HW, F32, None)
            tl = aalloc(HW, F32, None)
            mtg = aalloc(2 * T, BF16, [4, T])
            b_hal = abuf("hal")
            b_tl = abuf("tl")
            b_mtg = [[abuf(f"mtg{j}_{c}") for c in range(3)] for j in range(4)]
            base_pos = apos[0]
            sgt = aalloc(1024, F32, None)
            b_sgt = abuf("sgt")
            tls = hal
            b_tls = b_hal

            for ci in range(4):
                w = fetch(("in", l, "a", ci))
                proj_tm(w, t0p, 128, 0, ci * 128)
                proj_tm(w, t0s, 4, 1, ci * 128)
            P.op("act", R.activation(out=tl[:, TC:TC + 512], in_=ps[:, 0, :], func=AF.Copy), reads=[BANK[0]], writes=[b_tl])
            P.op("act", R.activation(out=tls[0:4, TC:TC + 512], in_=ps[0:4, 1, :], func=AF.Copy), reads=[BANK[1]], writes=[b_tls])
            for ci in range(4):
                w = fetch(("in", l, "g", ci))
                proj_tm(w, t0p, 128, 2, ci * 128)
                proj_tm(w, t0s, 4, 3, ci * 128)
            P.op("act", R.activation(out=sgt[:, 0:512], in_=ps[:, 2, :], func=AF.Sigmoid), reads=[BANK[2]], writes=[b_sgt])
            P.op("act", R.activation(out=sgt[0:4, 512:1024], in_=ps[0:4, 3, :], func=AF.Sigmoid), reads=[BANK[3]], writes=[b_sgt])
            P.op("dve", R.tensor_tensor(out=tl[:, TC:TC + 512], in0=tl[:, TC:TC + 512], in1=sgt[:, 0:512], op=ALU.mult),
                 reads=[b_tl, b_sgt], writes=[b_tl])
            P.op("dve", R.tensor_tensor(out=tls[0:4, TC:TC + 512], in0=tls[0:4, TC:TC + 512], in1=sgt[0:4, 512:1024], op=ALU.mult),
                 reads=[b_tls, b_sgt], writes=[b_tls])
            for ci in range(4):
                w = fetch(("in", l, "u", ci))
                proj_tm(w, t0p, 128, 0, ci * 128)
                proj_tm(w, t0s, 4, 1, ci * 128)
            P.op("act", R.activation(out=tl[:, TU:TU + 512], in_=ps[:, 0, :], func=AF.Copy), reads=[BANK[0]], writes=[b_tl])
            P.op("act", R.activation(out=tls[0:4, TU:TU + 512], in_=ps[0:4, 1, :], func=AF.Copy), reads=[BANK[1]], writes=[b_tls])
            for g in range(4):
                w = fetch(("in", l, "kd", g))
                proj_tm(w, t0p, 128, 2, g * 128)
                proj_tm(w, t0s, 4, 3, g * 128)
            P.op("dve", R.tensor_copy(out=tl[:, TK:TK + 512], in_=ps[:, 2, :]), reads=[BANK[2]], writes=[b_tl])
            P.op("dve", R.tensor_copy(out=tls[0:4, TK:TK + 512], in_=ps[0:4, 3, :]), reads=[BANK[3]], writes=[b_tls])
            for vu in range(2):
                w = fetch(("in", l, "v", vu))
                proj_tm(w, t0p, 128, 0, vu * 128)
                proj_tm(w, t0s, 4, 1, vu * 128)
            P.op("act", R.activation(out=tl[:, TV:TV + 256], in_=ps[:, 0, 0:256], func=AF.Copy), reads=[BANK[0]], writes=[b_tl])
            P.op("act", R.activation(out=tls[0:4, TV:TV + 256], in_=ps[0:4, 1, 0:256], func=AF.Copy), reads=[BANK[1]], writes=[b_tls])
            kt = int(os.environ.get("KT", "9"))
            if kt <= 1:
                return
            tlk = tl[:, TK:TK + 512].rearrange("p (g r d) -> p g r d", g=4, r=2)[:, :, 0, :]
            P.op("pool", R.dma_start(out=okp[l].rearrange("p (g d) -> p g d", g=4), in_=tlk), reads=[b_tl], stream=ST("tl"))
            P.op("pool", R.dma_start(out=ovp[l], in_=tl[:, TV:TV + 256]), reads=[b_tl], stream=ST("tl"))
            P.op("pool", R.dma_start(out=opp[l], in_=tl[113:128, TU:TU + 512]), reads=[b_tl], stream=ST("tl"))
            P.op("pool", R.dma_start(out=ocp[l], in_=tl[98:128, TC:TC + 512]), reads=[b_tl], stream=ST("tl"))
            SB_ = Buf("send")
            RB_ = Buf("recv")
            P.op("pool", R.dma_start(out=send[l][:, :], in_=tl[:, :]), reads=[b_tl], writes=[SB_], stream=ST("tl"))
            if kt <= 2:
                return
            P.op("pool", R.collective_compute("AllGather", ALU.bypass, replica_groups=[[0, 1], [2, 3], [4, 5], [6, 7]],
                                                        ins=[send[l].ap().opt()], outs=[recv[l].ap().opt()]),
                 reads=[SB_], writes=[RB_], stream=st_cc)
            if kt <= 3:
                return
            OKS = Buf("oks")
            for b in range(NS_TOK):
                P.op("pool", R.dma_start(out=oks[l, b, 0:127, :], in_=ck[l, b, 1:128, :]), writes=[OKS], stream=ST("oks"))
                P.op("pool", R.dma_start(out=ovs[l, b, 0:127, :], in_=cv[l, b, 1:128, :]), writes=[OKS], stream=ST("oks"))
                P.op("pool", R.dma_start(out=ops_[l, b, 0:14, :], in_=spool[l, b, 1:15, :]), stream=st_out2)
                P.op("pool", R.dma_start(out=ocs[l, b, 0:29, :], in_=sconv[l, b, 1:30, :]), stream=st_out2)
            for b in range(NS_TOK):
                P.op("pool", R.dma_start(out=ops_[l, b, 14:15, :], in_=tls[b:b + 1, TU:TU + 512]), reads=[b_tls], stream=ST("tls"))
                P.op("pool", R.dma_start(out=ocs[l, b, 29:30, :], in_=tls[b:b + 1, TC:TC + 512]), reads=[b_tls], stream=ST("tls"))
            for b in range(NS_TOK):
                tsk = tls[b:b + 1, TK:TK + 512].rearrange("p (g r d) -> p g r d", g=4, r=2)[:, :, 0, :]
                P.op("pool", R.dma_start(out=oks[l, b, 127:128, :].rearrange("p (g d) -> p g d", g=4), in_=tsk),
                     reads=[b_tls], writes=[OKS], stream=ST("oks"))
                P.op("pool", R.dma_start(out=ovs[l, b, 127:128, :], in_=tls[b:b + 1, TV:TV + 256]), reads=[b_tls], writes=[OKS], stream=ST("oks"))
            if kt <= 4:
                return
            P.op("pool", R.dma_start(out=hal[:, :], in_=recv[l][0:128, :]), reads=[RB_], writes=[b_hal], stream=ST("hal"))

            kmix = int(os.environ.get("KMIX", "9"))
            if kmix <= 1:
                return
            apos[0] = base_pos
            P.op("dve", R.memset(scratch1[:, 1:2], 0.0), writes=[b_sgt])
            f1 = P.ops["dve"][-1]

            def nbf(name, f):
                b_ = Buf(name)
                b_.w = f
                arena_bufs.append(b_)
                return b_
            cin = aalloc(4 * (30 + NPT), F32, [4, 30 + NPT])
            acc = aalloc(NPT, F32, None)
            cins = aalloc(4 * NS_TOK * 31, F32, None)
            cinsv = cins[:, :].rearrange("p (c b j) -> p c b j", c=4, b=NS_TOK)
            ycs = aalloc(4 * NS_TOK, F32, [4, NS_TOK])
            sg2 = aalloc(688, F32, [2, 344])
            scs = aalloc(512, F32, None)
            cvt = aalloc(NS_TOK * 31, F32, None)
            ycb = aalloc(344, BF16, [2, 344])
            ysq = aalloc(344, BF16, [2, 344])
            mean = aalloc(344, F32, None)
            var = aalloc(344, F32, None)
            rstd = aalloc(344, F32, None)
            t1 = aalloc(688, F32, [2, 344])
            ycn = aalloc(4 * 344 // 2, BF16, [4, 344])
            b_cin = [nbf(f"cin{ci}", f1) for ci in range(4)]
            b_acc, b_cinh, b_cins, b_scs, b_cvt, b_st = [nbf(n, f1) for n in ("acc", "cinh", "cins", "scs", "cvt", "cstt")]
            b_sg2 = [nbf("sg20", f1), nbf("sg21", f1)]
            b_ycb = [nbf("ycb0", f1), nbf("ycb1", f1)]
            b_ysq = [nbf("ysq0", f1), nbf("ysq1", f1)]
            b_t1 = [nbf("ct10", f1), nbf("ct11", f1)]
            b_ycn = [nbf(f"ycn{ci}", f1) for ci in range(4)]
            for ci in range(4):
                P.op("pe", R.transpose(ps[:, 2, ci * 128:(ci + 1) * 128], hal[:, TC + ci * 128:TC + (ci + 1) * 128], identf),
                     reads=[b_hal, CONSTB], writes=[BANK[2]])
            P.op("dve", R.tensor_scalar(out=cin[:, :, 0:30], in0=ps[:, 2, :].rearrange("p (c t) -> p c t", c=4)[:, :, 98:128],
                                                  scalar1=flag, scalar2=None, op0=ALU.mult),
                 reads=[BANK[2], CONSTB], writes=[b_cinh])
            P.op("pool", R.dma_start(out=scs[0:120, :], in_=sconv[l].rearrange("b r c -> (b r) c")), writes=[b_scs], stream=ST("scs"))
            for ci in range(4):
                P.op("pe", R.transpose(ps[:, 3, ci * 128:ci * 128 + 120], scs[0:120, ci * 128:(ci + 1) * 128], identf[0:120, 0:120]),
                     reads=[b_scs, CONSTB], writes=[BANK[3]])
            P.op("dve", R.tensor_copy(out=cinsv[:, :, :, 0:30],
                                                in_=ps[:, 3, :].rearrange("p (c x) -> p c x", c=4)[:, :, 0:120].rearrange("p c (b r) -> p c b r", b=NS_TOK)),
                 reads=[BANK[3]], writes=[b_cins])
            for ci in range(4):
                wg_ = fetch(("in", l, "g", ci))
                wa_ = fetch(("in", l, "a", ci))
                for c, (c0, c1) in enumerate(CGS):
                    n = c1 - c0
                    pair = rot["gu"] % 2
                    rot["gu"] += 1
                    bg, ba = 2 * pair, 2 * pair + 1
                    for k in range(16):
                        mm(ps[:, bg, 0:n], wbf[:, wg_, k * 128:(k + 1) * 128], hb[:, k, c0:c1], k == 0, k == 15, [WB[wg_], HBb[k][c]], [BANK[bg]])
                    for k in range(16):
                        mm(ps[:, ba, 0:n], wbf[:, wa_, k * 128:(k + 1) * 128], hb[:, k, c0:c1], k == 0, k == 15, [WB[wa_], HBb[k][c]], [BANK[ba]])
                    si = pair
                    P.op("act", R.activation(out=sg2[:, si, 0:n], in_=ps[:, bg, 0:n], func=AF.Sigmoid),
                         reads=[BANK[bg]], writes=[b_sg2[si]])
                    npr = min(c1, NPT) - c0
                    P.op("dve", R.tensor_tensor(out=cin[:, ci, 30 + c0:30 + c0 + npr], in0=sg2[:, si, 0:npr], in1=ps[:, ba, 0:npr], op=ALU.mult),
                         reads=[b_sg2[si], BANK[ba]], writes=[b_cin[ci]])
                    if c == 2:
                        P.op("dve", R.tensor_tensor(out=cinsv[:, ci, :, 30], in0=sg2[:, si, npr:npr + NS_TOK], in1=ps[:, ba, npr:npr + NS_TOK], op=ALU.mult),
                             reads=[b_sg2[si], BANK[ba]], writes=[b_cins])
            for ci in range(4):
                wj = lambda j, ci=ci: par[:, pl + PC_WDW + ci * 31 + j:pl + PC_WDW + ci * 31 + j + 1]
                bd = par[:, pl + PC_BDW + ci:pl + PC_BDW + ci + 1]
                rd = [b_cin[ci], b_cinh, PARB]
                P.op("dve", R.tensor_scalar(out=acc[:, 0:NPT], in0=cin[:, ci, 0:NPT], scalar1=wj(0), scalar2=bd,
                                                                            op0=ALU.mult, op1=ALU.add), reads=rd, writes=[b_acc])
                for j in range(1, 31):
                    P.op("dve", R.scalar_tensor_tensor(out=acc[:, 0:NPT], in0=cin[:, ci, j:j + NPT], scalar=wj(j),
                                                                                   in1=acc[:, 0:NPT], op0=ALU.mult, op1=ALU.add),
                         reads=rd + [b_acc], writes=[b_acc])
                P.op("pool", R.tensor_copy(out=cin[:, ci, 0:NPT], in_=acc[:, 0:NPT]), reads=[b_acc, b_cinh], writes=[b_cin[ci]])
                wrow = par[:, pl + PC_WDW + ci * 31:pl + PC_WDW + (ci + 1) * 31]
                P.op("dve", R.tensor_tensor(out=cvt[:, :].rearrange("p (b j) -> p b j", b=NS_TOK),
                                                                        in0=cinsv[:, ci, :, :],
                                                                        in1=wrow.unsqueeze(1).to_broadcast([128, NS_TOK, 31]), op=ALU.mult),
                     reads=[b_cins, PARB], writes=[b_cvt])
                P.op("dve", R.tensor_reduce(out=ycs[:, ci, :], in_=cvt[:, :].rearrange("p (b j) -> p b j", b=NS_TOK), axis=AX.X, op=ALU.add),
                     reads=[b_cvt], writes=[b_cins])
                P.op("dve", R.tensor_scalar(out=ycs[:, ci, :], in0=ycs[:, ci, :], scalar1=bd, scalar2=None, op0=ALU.add),
                     reads=[b_cins, PARB], writes=[b_cins])
            wpw = fetch(("pw", l))

            def ycv(ci, c0, c1):
                npr = min(c1, NPT) - c0
                out = [(cin[:, ci, c0:c0 + npr], 0, npr)]
                if c1 > NPT:
                    out.append((ycs[:, ci, :], npr, NS_TOK))
                return out
            for c, (c0, c1) in enumerate(CGS):
                n = c1 - c0
                for ci in range(4):
                    i = ci % 2
                    for (srcap, o0, wd_) in ycv(ci, c0, c1):
                        P.op("act", R.activation(out=ycb[:, i, o0:o0 + wd_], in_=srcap, func=AF.Copy),
                             reads=[b_cin[ci], b_cins], writes=[b_ycb[i]])
                        P.op("act", R.activation(out=ysq[:, i, o0:o0 + wd_], in_=srcap, func=AF.Square),
                             reads=[b_cin[ci], b_cins], writes=[b_ysq[i]])
                    mm(ps[:, 4, 0:n], onesb[:, :], ycb[:, i, 0:n], ci == 0, ci == 3, [b_ycb[i], CONSTB], [BANK[4]])
                    mm(ps[:, 5, 0:n], onesb[:, :], ysq[:, i, 0:n], ci == 0, ci == 3, [b_ysq[i], CONSTB], [BANK[5]])
                P.op("dve", R.tensor_scalar(out=mean[:, 0:n], in0=ps[:, 4, 0:n], scalar1=1.0 / 512, scalar2=None, op0=ALU.mult), reads=[BANK[4]], writes=[b_st])
                P.op("dve", R.tensor_tensor(out=var[:, 0:n], in0=mean[:, 0:n], in1=mean[:, 0:n], op=ALU.mult), reads=[b_st], writes=[b_st])
                P.op("dve", R.scalar_tensor_tensor(out=var[:, 0:n], in0=ps[:, 5, 0:n], scalar=1.0 / 512, in1=var[:, 0:n], op0=ALU.mult, op1=ALU.subtract),
                     reads=[BANK[5], b_st], writes=[b_st])
                P.op("dve", R.tensor_scalar(out=var[:, 0:n], in0=var[:, 0:n], scalar1=float(LN_EPS), scalar2=None, op0=ALU.add),
                     reads=[b_st], writes=[b_st])
                P.op("act", R.activation(out=var[:, 0:n], in_=var[:, 0:n], func=AF.Sqrt), reads=[b_st], writes=[b_st])
                P.op("dve", R.reciprocal(out=rstd[:, 0:n], in_=var[:, 0:n]), reads=[b_st], writes=[b_st])
                for ci in range(4):
                    i = ci % 2
                    gcl = par[:, pl + PC_CLG + ci:pl + PC_CLG + ci + 1]
                    bcl = par[:, pl + PC_CLB + ci:pl + PC_CLB + ci + 1]
                    for (srcap, o0, wd_) in ycv(ci, c0, c1):
                        P.op("pool", R.tensor_tensor(out=t1[:, i, o0:o0 + wd_], in0=srcap, in1=mean[:, o0:o0 + wd_], op=ALU.subtract),
                             reads=[b_cin[ci], b_cins, b_st], writes=[b_t1[i]])
                    P.op("dve", R.tensor_tensor(out=t1[:, i, 0:n], in0=t1[:, i, 0:n], in1=rstd[:, 0:n], op=ALU.mult),
                         reads=[b_t1[i], b_st], writes=[b_t1[i]])
                    P.op("act", R.activation(out=t1[:, i, 0:n], in_=t1[:, i, 0:n], func=AF.Identity, scale=gcl, bias=bcl),
                         reads=[b_t1[i], PARB], writes=[b_t1[i]])
                    P.op("act", R.activation(out=ycn[:, ci, 0:n], in_=t1[:, i, 0:n], func=AF.Silu), reads=[b_t1[i]], writes=[b_ycn[ci]])
                for dj in range(4):
                    bk = rot["gu"] % 4
                    rot["gu"] += 1
                    for ci in range(4):
                        mm(ps[:, bk, 0:n], wbf[:, wpw, ci * 512 + dj * 128:ci * 512 + (dj + 1) * 128], ycn[:, ci, 0:n], ci == 0, ci == 3,
                           [WB[wpw], b_ycn[ci]], [BANK[bk]])
                    P.op("act", R.activation(out=mtg[:, dj, c0:c1], in_=ps[:, bk, 0:n], func=AF.Copy),
                         reads=[BANK[bk]], writes=[b_mtg[dj][c]])
            wout_group(l, 1, mtg, b_mtg)

            if kmix <= 2:
                return
            P.op("dve", R.memset(scratch1[:, 2:3], 0.0),
                 writes=b_cin + [b_acc, b_cinh, b_cins, b_scs, b_cvt, b_st] + b_sg2 + b_ycb + b_ysq + b_t1 + b_ycn)
            f2 = P.ops["dve"][-1]
            apos[0] = base_pos
            L = 15 + NPT
            ue = aalloc(L + 1, F32, None)
            ua = aalloc(L + 1, F32, None)
            ub = aalloc(L + 1, F32, None)
            dd = aalloc(T // 2, BF16, None)
            dds = aalloc(8, F32, None)
            sps = aalloc(512, F32, None)
            uh = aalloc(4 * 16, F32, [4, 16])
            ush = aalloc(4 * NS_TOK * 16, F32, None)
            ushv = ush[:, :].rearrange("p (c b r) -> p c b r", c=4, b=NS_TOK)
            b_ue, b_ua, b_ub, b_dds, b_dd, b_sps, b_uh = [nbf(n, f2) for n in ("ue", "ua", "ub", "dds", "dd", "sps", "uh")]
            for ci in range(4):
                P.op("pe", R.transpose(ps[:, 2, ci * 128:(ci + 1) * 128], hal[:, TU + ci * 128:TU + (ci + 1) * 128], identf),
                     reads=[b_hal, CONSTB], writes=[BANK[2]])
            P.op("dve", R.tensor_scalar(out=uh[:, :, 0:15], in0=ps[:, 2, :].rearrange("p (c t) -> p c t", c=4)[:, :, 113:128],
                                                  scalar1=flag, scalar2=None, op0=ALU.mult),
                 reads=[BANK[2], CONSTB], writes=[b_uh])
            P.op("pool", R.dma_start(out=sps[0:60, :], in_=spool[l].rearrange("b r c -> (b r) c")), writes=[b_sps], stream=ST("sps"))
            for ci in range(4):
                P.op("pe", R.transpose(ps[:, 3, ci * 128:ci * 128 + 60], sps[0:60, ci * 128:(ci + 1) * 128], identf[0:60, 0:60]),
                     reads=[b_sps, CONSTB], writes=[BANK[3]])
            P.op("dve", R.tensor_copy(out=ushv[:, :, :, 0:15],
                                                in_=ps[:, 3, :].rearrange("p (c x) -> p c x", c=4)[:, :, 0:60].rearrange("p c (b r) -> p c b r", b=NS_TOK)),
                 reads=[BANK[3]], writes=[b_uh])
            wpool_slot = fetch(("wpool", l))
            wpl = aalloc(256, BF16, None)
            b_wpl = nbf("wpl", f2)
            P.op("dve", R.tensor_copy(out=wpl[:, :], in_=wbf[:, wpool_slot, 0:512]), reads=[WB[wpool_slot]], writes=[b_wpl])
            for ci in range(4):
                wn = 2 ** (ci + 1)
                wu_ = fetch(("in", l, "u", ci))

                def evu(c, c0, c1, n, bk, ci=ci):
                    npr = min(c1, NPT) - c0
                    P.op("act", R.activation(out=ue[:, 15 + c0:15 + c0 + npr], in_=ps[:, bk, 0:npr], func=AF.Copy), reads=[BANK[bk]], writes=[b_ue])
                    if c == 2:
                        P.op("act", R.activation(out=ushv[:, ci, :, 15], in_=ps[:, bk, npr:npr + NS_TOK], func=AF.Copy), reads=[BANK[bk]], writes=[b_uh])
                proj_fm(wu_, evu)
                P.op("dve", R.tensor_copy(out=ue[:, 0:15], in_=uh[:, ci, 0:15]), reads=[b_uh], writes=[b_ue])
                src, srcb = ue, b_ue
                dsts = [(ua, b_ua), (ub, b_ub)]
                step = 1
                di = 0
                while step < wn:
                    dst, dstb = dsts[di % 2]
                    lo = 2 * step - 1
                    P.op("dve", R.tensor_tensor(out=dst[:, lo:L], in0=src[:, lo:L], in1=src[:, lo - step:L - step], op=ALU.add),
                         reads=[srcb], writes=[dstb])
                    src, srcb = dst, dstb
                    di += 1
                    step *= 2
                oth, othb = (ub, b_ub) if src is ua else (ua, b_ua)
                P.op("dve", R.scalar_tensor_tensor(out=dd[:, 15:NPT], in0=src[:, 30:L], scalar=1.0 / wn, in1=ue[:, 30:L],
                                                                      op0=ALU.mult, op1=ALU.subtract),
                     reads=[srcb, b_ue], writes=[b_dd])
                P.op("dve", R.tensor_tensor(out=oth[:, 0:15], in0=src[:, 15:30],
                                                                               in1=invc.rearrange("p (g t) -> p g t", g=4)[:, ci, 0:15], op=ALU.mult),
                     reads=[srcb, CONSTB], writes=[othb])
                P.op("dve", R.tensor_tensor(out=dd[:, 0:15], in0=oth[:, 0:15], in1=ue[:, 15:30], op=ALU.subtract),
                     reads=[othb, b_ue], writes=[b_dd])
                P.op("dve", R.tensor_reduce(out=dds[:, 0:NS_TOK], in_=ushv[:, ci, :, 16 - wn:16], axis=AX.X, op=ALU.add),
                     reads=[b_uh], writes=[b_dds])
                P.op("dve", R.scalar_tensor_tensor(out=dd[:, NPT:T], in0=dds[:, 0:NS_TOK], scalar=1.0 / wn, in1=ushv[:, ci, :, 15],
                                                                           op0=ALU.mult, op1=ALU.subtract),
                     reads=[b_dds, b_uh], writes=[b_dd])
                psc = par[:, pl + PC_PS + ci:pl + PC_PS + ci + 1]
                for c, (c0, c1) in enumerate(CGS):
                    n = c1 - c0
                    bk = rot["gu"] % 4
                    rot["gu"] += 1
                    mm(ps[:, bk, 0:n], wpl[:, ci * 128:(ci + 1) * 128], dd[:, c0:c1], True, True, [b_wpl, b_dd], [BANK[bk]])
                    P.op("act", R.activation(out=mtg[:, ci, c0:c1], in_=ps[:, bk, 0:n], func=AF.Copy, scale=psc),
                         reads=[BANK[bk], PARB], writes=[b_mtg[ci][c]])
            wout_group(l, 0, mtg, b_mtg)

            if kmix <= 3:
                return
            P.op("dve", R.memset(scratch1[:, 3:4], 0.0), writes=[b_ue, b_ua, b_ub, b_dds, b_dd, b_sps, b_uh, b_tl, b_wpl])
            f3 = P.ops["dve"][-1]
            apos[0] = base_pos - 2 * T - HW
            tl_off = HW
            apos[0] = tl_off
            halb = aalloc(256, BF16, None)
            knb = aalloc(256, BF16, None)
            kn = aalloc(256, F32, None)
            vn = aalloc(256, F32, None)
            stat = aalloc(32, F32, [2, 16])
            sst = aalloc(96, F32, None)
            qm = aalloc(8 * NS_TOK * NS_TOK // 2, BF16, [8, NS_TOK, NS_TOK])
            qsall = aalloc(8 * NS_TOK // 2, BF16, [8, NS_TOK])
            spt = aalloc(8 * NS_TOK // 2, BF16, [8, NS_TOK])
            assert apos[0] <= 2 * HW, apos[0]
            apos[0] = base_pos
            kT = aalloc(4 * (128 + T) // 2, BF16, [4, 128 + T])
            vd = aalloc(10 * 512 // 2, BF16, [10, 512])
            kTs = aalloc(NS_TOK * 4 * 128 // 2, BF16, [NS_TOK, 4, 128])
            vds = aalloc(NS_TOK * 512 // 2, BF16, [NS_TOK, 512])
            qT = aalloc(T, BF16, [2, T])
            bh = aalloc(512, F32, [2, 256])
            sbt = aalloc(512, F32, [2, 256])
            pbt = aalloc(256, BF16, [2, 256])
            ptt = aalloc(256, BF16, [2, 256])
            bss = aalloc(8 * 128, F32, [8, 128])
            bhs = aalloc(256, F32, [2, 128])
            save_pos = apos[0]
            apos[0] = 0
            ssb = aalloc(8 * 128, F32, [8, 128])
            spb = aalloc(8 * 128 // 2, BF16, [8, 128])
            apos[0] = save_pos
            nb = lambda name: nbf(name, f3)
            b_kT = [nb(f"kT{g}") for g in range(4)]
            b_kh = nb("kTh")
            b_vd = [nb(f"vd{i}") for i in range(10)]
            b_halb, b_knb, b_kn, b_vn, b_kTs, b_vds = [nb(n) for n in ("halb", "knb", "kn", "vn", "kTs", "vds")]
            b_q = [nb("q0"), nb("q1")]
            b_bh = [nb("bh0"), nb("bh1")]
            b_bhs = [nb("bhs0"), nb("bhs1")]
            b_sb = [nb("sb0"), nb("sb1")]
            b_pb_ = [nb("pb0"), nb("pb1")]
            b_pt = [nb("pt0"), nb("pt1")]
            b_stt = [nb("st0"), nb("st1")]
            b_qm, b_qs, b_sst, b_bss, b_spt = [nb(n) for n in ("qm", "qs", "sst", "bss", "spt")]
            b_ssb = b_hal
            b_spb = b_hal
            for g in range(4):
                w = fetch(("in", l, "kd", g))

                def ev(c, c0, c1, n, bk, g=g):
                    P.op("act", R.activation(out=kT[:, g, 128 + c0:128 + c1], in_=ps[:, bk, 0:n], func=AF.Copy),
                         reads=[BANK[bk]], writes=[b_kT[g]])
                proj_fm(w, ev)
            for vu in range(2):
                w = fetch(("in", l, "v", vu))
                for tb in range(8):
                    bk = 4 + (vu * 8 + tb) % 4
                    proj_tm(w, tb * 128, 128, bk, 0)
                    for dup in range(2):
                        P.op("dve" if dup == 0 else "act",
                             (R.tensor_copy(out=vd[:, tb + 1, vu * 256:(vu + 1) * 256].rearrange("p (g r d) -> p g r d", g=2, r=2)[:, :, dup, :],
                                            in_=ps[:, bk, 0:128].rearrange("p (g d) -> p g d", g=2))) if dup == 0 else
                             (R.activation(out=vd[:, tb + 1, vu * 256:(vu + 1) * 256].rearrange("p (g r d) -> p g r d", g=2, r=2)[:, :, dup, :],
                                           in_=ps[:, bk, 0:128].rearrange("p (g d) -> p g d", g=2), func=AF.Copy)),
                             reads=[BANK[bk]], writes=[b_vd[tb + 1]])
            P.op("dve", R.tensor_copy(out=halb[:, :], in_=hal[:, TK:TK + 512]), reads=[b_hal], writes=[b_halb])
            for g in range(4):
                P.op("pe", R.transpose(psb[:, 0, g * 128:(g + 1) * 128], halb[:, g * 128:(g + 1) * 128], identb[:, :]),
                     reads=[b_halb, CONSTB], writes=[BANK[0]])
            P.op("dve", R.tensor_copy(out=kT[:, :, 0:128], in_=psb[:, 0, 0:512].rearrange("p (g t) -> p g t", g=4)),
                 reads=[BANK[0]], writes=[b_kh])
            for dup in range(2):
                P.op("dve", R.tensor_copy(out=vd[:, 0, :].rearrange("p (g r d) -> p g r d", g=4, r=2)[:, :, dup, :],
                                                             in_=hal[:, TV:TV + 256].rearrange("p (g d) -> p g d", g=4)),
                     reads=[b_hal], writes=[b_vd[0]])
            for b in range(NS_TOK):
                P.op("pool", R.dma_start(out=kn[:, :], in_=oks[l, b]), reads=[OKS], writes=[b_kn], stream=ST("kn"))
                P.op("pool", R.dma_start(out=vn[:, :], in_=ovs[l, b]), reads=[OKS], writes=[b_vn], stream=ST("vn"))
                for dup in range(2):
                    P.op("dve", R.tensor_copy(out=knb[:, :].rearrange("p (g r d) -> p g r d", g=4, r=2)[:, :, dup, :],
                                                                 in_=kn[:, :].rearrange("p (g d) -> p g d", g=4)),
                         reads=[b_kn], writes=[b_knb])
                    P.op("dve", R.tensor_copy(out=vds[:, b, :].rearrange("p (g r d) -> p g r d", g=4, r=2)[:, :, dup, :],
                                                                      in_=vn[:, :].rearrange("p (g d) -> p g d", g=4)),
                         reads=[b_vn], writes=[b_vds])
                for g in range(4):
                    P.op("pe", R.transpose(psb[:, 1, g * 128:(g + 1) * 128], knb[:, g * 128:(g + 1) * 128], identb[:, :]),
                         reads=[b_knb, CONSTB], writes=[BANK[1]])
                P.op("act", R.activation(out=kTs[:, b, :, :], in_=psb[:, 1, 0:512].rearrange("p (g t) -> p g t", g=4), func=AF.Copy),
                     reads=[BANK[1]], writes=[b_kTs])

            def sample_attn(half):
                h0 = 8 * half
                P.op("pool", R.dma_start(out=bss[0:NS_TOK, :, :], in_=bass.AP(gd.ap().tensor, 128 + h0 * GW, [[0, NS_TOK], [GW, 8], [1, 128]])),
                     reads=[BIASB], writes=[b_bss], stream=ST("bss"))
                for b in range(NS_TOK):
                    P.op("dve", R.tensor_copy(out=qm[:, h0 // 2:h0 // 2 + 4, b, b], in_=qsall[:, h0 // 2:h0 // 2 + 4, b]), reads=[b_qs], writes=[b_qm])
                for hi in range(8):
                    hh = h0 + hi
                    cq, ep, g = hh // 2, hh % 2, hh // 4
                    r0 = 64 * ep
                    bk = hi // 4
                    for b in range(NS_TOK):
                        mm(ps[0:NS_TOK, bk, (hi % 4) * 128:(hi % 4 + 1) * 128], qm[r0:r0 + 64, cq, b, :], kTs[r0:r0 + 64, b, g, :], b == 0, b == NS_TOK - 1,
                           [b_qm, b_kTs], [BANK[bk]])
                for bk in range(2):
                    P.op("dve", R.scalar_tensor_tensor(out=ssb[0:NS_TOK, 4 * bk:4 * bk + 4, :].rearrange("p a b -> p (a b)"), in0=ps[0:NS_TOK, bk, :], scalar=float(SCALE),
                                                                        in1=bss[0:NS_TOK, 4 * bk:4 * bk + 4, :].rearrange("p a b -> p (a b)"), op0=ALU.mult, op1=ALU.add),
                         reads=[BANK[bk], b_bss], writes=[b_ssb])
                s4 = lambda a: sst[0:NS_TOK, a * 8:(a + 1) * 8]
                sk = par[0:NS_TOK, pl + PC_SINK + h0:pl + PC_SINK + h0 + 8]
                P.op("dve", R.tensor_reduce(out=s4(0), in_=ssb[0:NS_TOK, :, :], axis=AX.X, op=ALU.max), reads=[b_ssb], writes=[b_sst])
                P.op("dve", R.tensor_tensor(out=s4(1), in0=s4(0), in1=sk, op=ALU.max), reads=[b_sst, PARB], writes=[b_sst])
                P.op("dve", R.tensor_scalar(out=s4(6), in0=s4(1), scalar1=-1.0, scalar2=None, op0=ALU.mult), reads=[b_sst], writes=[b_sst])
                for hi in range(8):
                    P.op("act", R.activation(out=ssb[0:NS_TOK, hi, :], in_=ssb[0:NS_TOK, hi, :], func=AF.Exp, bias=s4(6)[:, hi:hi + 1]),
                         reads=[b_ssb, b_sst], writes=[b_ssb])
                P.op("dve", R.tensor_reduce(out=s4(2), in_=ssb[0:NS_TOK, :, :], axis=AX.X, op=ALU.add), reads=[b_ssb], writes=[b_sst])
                P.op("dve", R.tensor_tensor(out=s4(3), in0=sk, in1=s4(1), op=ALU.subtract), reads=[b_sst, PARB], writes=[b_sst])
                P.op("act", R.activation(out=s4(3), in_=s4(3), func=AF.Exp), reads=[b_sst], writes=[b_sst])
                P.op("dve", R.tensor_tensor(out=s4(4), in0=s4(2), in1=s4(3), op=ALU.add), reads=[b_sst], writes=[b_sst])
                P.op("dve", R.reciprocal(out=s4(5), in_=s4(4)), reads=[b_sst], writes=[b_sst])
                for hi in range(8):
                    P.op("dve", R.tensor_scalar(out=spb[0:NS_TOK, hi, :], in0=ssb[0:NS_TOK, hi, :], scalar1=s4(5)[:, hi:hi + 1], scalar2=None, op0=ALU.mult),
                         reads=[b_ssb, b_sst], writes=[b_spb])
                ks = int(os.environ.get("KS", "9"))
                if ks <= 1:
                    return
                for hi in range(8):
                    P.op("pe", R.transpose(psb[:, 2, hi * NS_TOK:(hi + 1) * NS_TOK], spb[0:NS_TOK, hi, :], identb[0:NS_TOK, 0:NS_TOK]),
                         reads=[b_spb, CONSTB], writes=[BANK[2]])
                P.op("act", R.activation(out=spt[:, :, :], in_=psb[:, 2, 0:8 * NS_TOK].rearrange("p (h b) -> p h b", h=8), func=AF.Copy),
                     reads=[BANK[2]], writes=[b_spt])
                if ks <= 2:
                    return
                for b in range(NS_TOK):
                    for gi in range(2):
                        g = h0 // 4 + gi
                        mm(ps[:, 3, b * 8 + gi * 4:b * 8 + gi * 4 + 4], vds[:, b, g * 128:(g + 1) * 128], spt[:, gi * 4:gi * 4 + 4, b], True, True,
                           [b_vds, b_spt], [BANK[3]])
                if ks <= 3:
                    return
                for ep in range(2):
                    r0 = 64 * ep
                    src = ps[r0:r0 + 64, 3, 0:8 * NS_TOK].rearrange("p (b m e) -> p b m e", b=NS_TOK, m=4)[:, :, :, ep].rearrange("p b m -> p m b")
                    P.op("dve", R.tensor_copy(out=mtg[r0:r0 + 64, :, NPT:T], in_=src),
                         reads=[BANK[3]], writes=[b_mtg[j][2] for j in range(4)])

            ucnt = 0
            ka = int(os.environ.get("KA", "9"))
            for cq in range(8 if ka > 1 else 0):
                qi = cq % 2
                wq = fetch(("in", l, "q", cq))

                def evq(c, c0, c1, n, bk, qi=qi):
                    P.op("act", R.activation(out=qT[:, qi, c0:c1], in_=ps[:, bk, 0:n], func=AF.Copy), reads=[BANK[bk]], writes=[b_q[qi]])
                proj_fm(wq, evq)
                mj = cq % 4
                for ep in range(2):
                    hh = 2 * cq + ep
                    g = hh // 4
                    r0 = 64 * ep
                    bi = hh % 2
                    P.op("pool", R.dma_start(out=bh[:, bi, :], in_=bass.AP(wsk.ap().tensor, hh * 128 * GW + 127, [[GW - 1, 128], [1, 256]])),
                         reads=[BIASB], writes=[b_bh[bi]], stream=ST(f"bh{bi}"))
                    for blk in range(8):
                        u = ucnt % 2
                        ucnt += 1
                        bS = u
                        q0 = blk * 128
                        mm(ps[:, bS, 0:256], qT[r0:r0 + 64, qi, q0:q0 + 128], kT[r0:r0 + 64, g, q0:q0 + 256], True, True,
                           [b_q[qi], b_kT[g], b_kh], [BANK[bS]])
                        P.op("dve", R.scalar_tensor_tensor(out=sbt[:, u, :], in0=ps[:, bS, 0:256], scalar=float(SCALE), in1=bh[:, bi, :],
                                                                                       op0=ALU.mult, op1=ALU.add),
                             reads=[BANK[bS], b_bh[bi]], writes=[b_sb[u]])
                        if blk == 0:
                            P.op("dve", R.tensor_scalar(out=sbt[:, u, 0:128], in0=sbt[:, u, 0:128], scalar1=negflag, scalar2=None, op0=ALU.add),
                                 reads=[b_sb[u], CONSTB], writes=[b_sb[u]])
                        st_ = stat[:, u, :]
                        P.op("dve", R.tensor_reduce(out=st_[:, 0:1], in_=sbt[:, u, :], axis=AX.X, op=ALU.max), reads=[b_sb[u]], writes=[b_stt[u]])
                        P.op("dve", R.tensor_scalar(out=st_[:, 1:2], in0=st_[:, 0:1], scalar1=sink_ap(hh), scalar2=-1.0, op0=ALU.max, op1=ALU.mult),
                             reads=[b_stt[u], PARB], writes=[b_stt[u]])
                        P.op("act", R.activation(out=sbt[:, u, :], in_=sbt[:, u, :], func=AF.Exp, bias=st_[:, 1:2]),
                             reads=[b_sb[u], b_stt[u]], writes=[b_sb[u]])
                        P.op("dve", R.tensor_reduce(out=st_[:, 2:3], in_=sbt[:, u, :], axis=AX.X, op=ALU.add), reads=[b_sb[u]], writes=[b_stt[u]])
                        P.op("act", R.activation(out=st_[:, 3:4], in_=st_[:, 1:2], func=AF.Exp, bias=sink_ap(hh)),
                             reads=[b_stt[u], PARB], writes=[b_stt[u]])
                        P.op("dve", R.tensor_tensor(out=st_[:, 4:5], in0=st_[:, 2:3], in1=st_[:, 3:4], op=ALU.add), reads=[b_stt[u]], writes=[b_stt[u]])
                        P.op("dve", R.reciprocal(out=st_[:, 5:6], in_=st_[:, 4:5]), reads=[b_stt[u]], writes=[b_stt[u]])
                        P.op("dve", R.tensor_scalar(out=pbt[:, u, :], in0=sbt[:, u, :], scalar1=st_[:, 5:6], scalar2=None, op0=ALU.mult),
                             reads=[b_sb[u], b_stt[u]], writes=[b_pb_[u]])
                        bT = 2 + u
                        for hf in range(2):
                            P.op("pe", R.transpose(psb[:, bT, hf * 128:(hf + 1) * 128], pbt[:, u, hf * 128:(hf + 1) * 128], identb[:, :]),
                                 reads=[b_pb_[u], CONSTB], writes=[BANK[bT]])
                        P.op("act", R.activation(out=ptt[:, u, :], in_=psb[:, bT, 0:256], func=AF.Copy), reads=[BANK[bT]], writes=[b_pt[u]])
                        bO = 4 + (hh % 2) * 2 + (blk // 4)
                        oc = (blk % 4) * 128
                        for hf in range(2):
                            mm(ps[:, bO, oc:oc + 128], vd[:, blk + hf, g * 128:(g + 1) * 128], ptt[:, u, hf * 128:(hf + 1) * 128], hf == 0, hf == 1,
                               [b_vd[blk + hf], b_pt[u]], [BANK[bO]])
                        if blk % 4 == 3:
                            cc0 = (blk - 3) * 128
                            touch = [c for c, (a0, a1) in enumerate(CGS) if a0 < cc0 + 512 and a1 > cc0]
                            P.op("act", R.activation(out=mtg[r0:r0 + 64, mj, cc0:cc0 + 512], in_=ps[r0:r0 + 64, bO, :], func=AF.Copy),
                                 reads=[BANK[bO]], writes=[b_mtg[mj][c] for c in touch])
                for ep in range(2 if ka > 2 else 0):
                    hh = 2 * cq + ep
                    g = hh // 4
                    r0 = 64 * ep
                    bi = hh % 2
                    P.op("pool", R.dma_start(out=bhs[:, bi, :], in_=bass.AP(gd.ap().tensor, hh * GW + 128, [[0, 128], [1, 128]])),
                         reads=[BIASB], writes=[b_bhs[bi]], stream=ST(f"bhs{bi}"))
                    bOs = 4 + (hh % 2) * 2
                    for b in range(NS_TOK):
                        u = ucnt % 2
                        ucnt += 1
                        bS = u
                        mm(ps[:, bS, 0:128], qT[r0:r0 + 64, qi, T - 128:T], kTs[r0:r0 + 64, b, g, :], True, True,
                           [b_q[qi], b_kTs], [BANK[bS]])
                        P.op("dve", R.scalar_tensor_tensor(out=sbt[:, u, 0:128], in0=ps[:, bS, 0:128], scalar=float(SCALE), in1=bhs[:, bi, :],
                                                           op0=ALU.mult, op1=ALU.add),
                             reads=[BANK[bS], b_bhs[bi]], writes=[b_sb[u]])
                        st_ = stat[:, u, :]
                        P.op("dve", R.tensor_reduce(out=st_[:, 0:1], in_=sbt[:, u, 0:128], axis=AX.X, op=ALU.max), reads=[b_sb[u]], writes=[b_stt[u]])
                        P.op("dve", R.tensor_scalar(out=st_[:, 1:2], in0=st_[:, 0:1], scalar1=sink_ap(hh), scalar2=-1.0, op0=ALU.max, op1=ALU.mult),
                             reads=[b_stt[u], PARB], writes=[b_stt[u]])
                        P.op("act", R.activation(out=sbt[:, u, 0:128], in_=sbt[:, u, 0:128], func=AF.Exp, bias=st_[:, 1:2]),
                             reads=[b_sb[u], b_stt[u]], writes=[b_sb[u]])
                        P.op("dve", R.tensor_reduce(out=st_[:, 2:3], in_=sbt[:, u, 0:128], axis=AX.X, op=ALU.add), reads=[b_sb[u]], writes=[b_stt[u]])
                        P.op("act", R.activation(out=st_[:, 3:4], in_=st_[:, 1:2], func=AF.Exp, bias=sink_ap(hh)),
                             reads=[b_stt[u], PARB], writes=[b_stt[u]])
                        P.op("dve", R.tensor_tensor(out=st_[:, 4:5], in0=st_[:, 2:3], in1=st_[:, 3:4], op=ALU.add), reads=[b_stt[u]], writes=[b_stt[u]])
                        P.op("dve", R.reciprocal(out=st_[:, 5:6], in_=st_[:, 4:5]), reads=[b_stt[u]], writes=[b_stt[u]])
                        P.op("dve", R.tensor_scalar(out=pbt[:, u, 0:128], in0=sbt[:, u, 0:128], scalar1=st_[:, 5:6], scalar2=None, op0=ALU.mult),
                             reads=[b_sb[u], b_stt[u]], writes=[b_pb_[u]])
                        bT = 2 + u
                        P.op("pe", R.transpose(psb[:, bT, 0:128], pbt[:, u, 0:128], identb[:, :]), reads=[b_pb_[u], CONSTB], writes=[BANK[bT]])
                        P.op("act", R.activation(out=ptt[:, u, 0:128], in_=psb[:, bT, 0:128], func=AF.Copy), reads=[BANK[bT]], writes=[b_pt[u]])
                        mm(ps[:, bOs, b * 128:(b + 1) * 128], vds[:, b, g * 128:(g + 1) * 128], ptt[:, u, 0:128], True, True,
                           [b_vds, b_pt[u]], [BANK[bOs]])
                    for b in range(NS_TOK):
                        col = b * 128 + 124 + b
                        P.op("act", R.activation(out=mtg[r0:r0 + 64, mj, NPT + b:NPT + b + 1], in_=ps[r0:r0 + 64, bOs, col:col + 1], func=AF.Copy),
                             reads=[BANK[bOs]], writes=[b_mtg[mj][2]])
                if cq % 4 == 3:
                    wout_group(l, 2 + cq // 4, mtg, b_mtg)

        kstop = int(os.environ.get("KSTOP", "99"))
        for l in range(n_layers):
            stages = [lambda: ffn(l, 0), lambda: layer_norm(l, 0, True), lambda: mixer(l), lambda: layer_norm(l, 1, True),
                      lambda: ffn(l, 1), lambda: layer_norm(l, 2, False), lambda: ple(l, l == n_layers - 1)]
            for si, st_fn in enumerate(stages):
                if l * 7 + si + 1 > kstop:
                    break
                st_fn()
        for k in range(16):
            P.op("pool", R.dma_start(out=yT[k], in_=h[:, k, :]), reads=[Hb[k][0], Hb[k][1], Hb[k][2]], stream=st_out)

        for e_ in Prog.ENGS:
            cnt_ = 0
            for o in P.ops[e_]:
                if o.stream is None and o.signal:
                    cnt_ += 1
                    o.semval = cnt_

        def emit_one(name, eng):
            for o in P.ops[name]:
                for d in o.waits:
                    if d.stream is not None:
                        eng.wait_ge(d.stream.sem, d.stream.inc * (d.seq + 1))
                    else:
                        eng.wait_ge(P.sem[d.eng], d.semval)
                ins = getattr(eng, o.fn[0])(*o.fn[1], **o.fn[2])
                if o.stream is not None:
                    if o.stream.inc == 16:
                        ins.then_inc(o.stream.sem, 16)
                    else:
                        ins.then_inc(o.stream.sem)
                elif o.signal:
                    ins.then_inc(P.sem[name], 1)
            if name == "pool":
                for s in final_streams + list(_sn.values()):
                    if s.count > 0:
                        eng.wait_ge(s.sem, s.inc * s.count)

        assert len(specs) + LOOKA + NSTG + 2 <= NU_ALLOC, (len(specs), NU_ALLOC)
        block.sync(lambda e: emit_one("sp", e))
        block.scalar(lambda e: emit_one("act", e))
        block.vector(lambda e: emit_one("dve", e))
        block.gpsimd(lambda e: emit_one("pool", e))
        block.tensor(lambda e: emit_one("pe", e))
        print("ops:", {k: len(v) for k, v in P.ops.items()}, "units", len(specs), flush=True)
    return nc, specs


NU_TOTAL = DEPTH * 344 + LOOKA + NSTG + 2
_CACHE = {}


def _colunit(W, c0, ncols=128):
    K = W.shape[0]
    blk = W[:, c0:c0 + ncols].reshape(K // 128, 128, ncols).transpose(1, 0, 2).reshape(128, -1)
    return blk


def _make_unit(spec, w):
    kind = spec[0]
    out = np.zeros((128, UW), np.float32)
    if kind in ("gate", "up"):
        _, l, which, j = spec
        W = w[f"ffn{which + 1}_w_{kind}"][l]
        out[:] = _colunit(W, j * 128)
    elif kind == "down":
        _, l, which, j = spec
        out[:] = w[f"ffn{which + 1}_w_down"][l][j * 128:(j + 1) * 128, :]
    elif kind == "in":
        _, l, k2, idx = spec
        W = w["w_in"][l]
        if k2 == "u":
            out[:] = _colunit(W, idx * 128)
        elif k2 == "a":
            out[:] = _colunit(W, 512 + idx * 128)
        elif k2 == "g":
            out[:] = _colunit(W, 1024 + idx * 128)
        elif k2 == "q":
            out[:] = _colunit(W, 1536 + idx * 128)
        elif k2 == "kd":
            kk = _colunit(W, 2560 + idx * 64, 64).reshape(128, 16, 64)
            out[:] = np.concatenate([kk, kk], axis=2).reshape(128, UW)
        elif k2 == "v":
            out[:] = _colunit(W, 2816 + idx * 128)
    elif kind == "wout":
        _, l, r = spec
        out[:] = w["w_out"][l][r * 128:(r + 1) * 128, :]
    elif kind == "pw":
        _, l = spec
        out[:] = w["w_pw"][l].reshape(4, 128, 512).transpose(1, 0, 2).reshape(128, UW)
    elif kind == "wpool":
        _, l = spec
        out[:, 0:512] = w["w_pool"][l].transpose(1, 0, 2).reshape(128, 512)
    elif kind == "ple":
        _, l, i = spec
        out[:] = w["w_ple"][l][:, i * 1024:(i + 1) * 1024].reshape(2, 128, 1024).transpose(1, 0, 2).reshape(128, UW)
    elif kind == "pg":
        _, l, dq = spec
        out[:] = _colunit(w["w_ple_gate"][l], dq * 128)
    else:
        raise ValueError(kind)
    return out


def _fm(v, n):
    return np.asarray(v, np.float32).reshape(n, 128).T


def kernel(**inputs):
    w = {k: np.asarray(v) for k, v in inputs.items()}
    nl = int(os.environ.get("KDEPTH", DEPTH))
    nu_total = nl * 344 + LOOKA + NSTG + 2
    if "nc" not in _CACHE:
        _CACHE["nc"], _CACHE["specs"] = build(n_layers=nl, nunits_hint=nu_total)
    nc, specs = _CACHE["nc"], _CACHE["specs"]
    wstream = np.zeros((nu_total, 128, UW), np.float32)
    for i, sp_ in enumerate(specs):
        wstream[i] = _make_unit(sp_, w)
    params = np.zeros((128, DEPTH * PC_N), np.float32)
    for l in range(DEPTH):
        o = l * PC_N
        for i, nm in enumerate(["ln1_g", "ln1_b", "ln2_g", "ln2_b", "ln3_g", "ln3_b"]):
            params[:, o + PC_LN + 16 * i:o + PC_LN + 16 * (i + 1)] = _fm(w[nm][l], 16)
        params[:, o + PC_PS:o + PC_PS + 4] = _fm(w["pool_scale"][l], 4)
        params[:, o + PC_BDW:o + PC_BDW + 4] = _fm(w["b_dw"][l], 4)
        params[:, o + PC_CLG:o + PC_CLG + 4] = _fm(w["conv_ln_g"][l], 4)
        params[:, o + PC_CLB:o + PC_CLB + 4] = _fm(w["conv_ln_b"][l], 4)
        params[:, o + PC_WDW:o + PC_WDW + 124] = w["w_dw"][l].T.reshape(4, 128, 31).transpose(1, 0, 2).reshape(128, 124)
        params[:, o + PC_SINK:o + PC_SINK + 16] = np.broadcast_to(w["sinks"][l][None, :], (128, 16))
    relaug = np.concatenate([w["rel_bias"].astype(np.float32), np.ones((1, 16), np.float32)], 0)
    ohaug = np.zeros((33, GW), np.float32)
    bt = _bucket_table()
    for m in range(GW):
        dist = 255 - m
        if 0 <= dist < 128:
            ohaug[bt[dist], m] = 1.0
        else:
            ohaug[32, m] = -1e30
    in_maps = []
    for c in range(8):
        s, half = c // 2, c % 2
        xs = np.concatenate([w["x_prompt"][s, half * NPT:(half + 1) * NPT, :], w["x_sample"][4 * c:4 * c + 4, 0, :]], 0)
        xT = np.ascontiguousarray(xs.T).reshape(16, 128, T)
        pTs = []
        for l in range(DEPTH):
            pp = np.concatenate([w["p_prompt"][l, s, half * NPT:(half + 1) * NPT, :], w["p_sample"][l, 4 * c:4 * c + 4, 0, :]], 0)
            pTs.append(np.ascontiguousarray(pp.T).reshape(2, 128, T))
        cst = np.zeros((128, 208), np.float32)
        cst[:, 0:128] = np.eye(128, dtype=np.float32)
        cst[:, 128] = float(half)
        cst[:, 129] = 0.0 if half == 1 else -1e30
        for g in range(4):
            wn = 2 ** (g + 1)
            for t in range(16):
                cst[:, 144 + g * 16 + t] = 1.0 / (min(t + 1, wn) if half == 0 else wn)
        in_maps.append({
            "xT": xT.astype(np.float32), "pT": np.stack(pTs).astype(np.float32), "wstream": wstream, "params": params,
            "ck": np.ascontiguousarray(w["cache_k"][:, 4 * c:4 * c + 4].reshape(DEPTH, 4, 128, 256)),
            "cv": np.ascontiguousarray(w["cache_v"][:, 4 * c:4 * c + 4].reshape(DEPTH, 4, 128, 256)),
            "spool": np.ascontiguousarray(w["state_pool"][:, 4 * c:4 * c + 4]),
            "sconv": np.ascontiguousarray(w["state_conv"][:, 4 * c:4 * c + 4]),
            "relaug": relaug, "ohaug": ohaug, "cst": cst,
        })
    res = run_bass_kernel_spmd(nc, in_maps, core_ids=list(range(8)))
    R = res.results
    y_prompt = np.zeros((4, 2048, D), np.float32)
    y_sample = np.zeros((32, 1, D), np.float32)
    nkp = np.zeros((DEPTH, 4, 128, 4, 64), np.float32)
    nvp = np.zeros_like(nkp)
    npp = np.zeros((DEPTH, 4, 15, 512), np.float32)
    ncp = np.zeros((DEPTH, 4, 30, 512), np.float32)
    nks = np.zeros((DEPTH, 32, 128, 4, 64), np.float32)
    nvs = np.zeros_like(nks)
    nps = np.zeros((DEPTH, 32, 15, 512), np.float32)
    ncs = np.zeros((DEPTH, 32, 30, 512), np.float32)
    for c in range(8):
        s, half = c // 2, c % 2
        r = R[c]
        yt = np.asarray(r["yT"]).reshape(D, T)
        y_prompt[s, half * NPT:(half + 1) * NPT, :] = yt[:, :NPT].T
        y_sample[4 * c:4 * c + 4, 0, :] = yt[:, NPT:].T
        if half == 1:
            nkp[:, s] = np.asarray(r["okp"]).reshape(DEPTH, 128, 4, 64)
            nvp[:, s] = np.asarray(r["ovp"]).reshape(DEPTH, 128, 4, 64)
            npp[:, s] = np.asarray(r["opp"])
            ncp[:, s] = np.asarray(r["ocp"])
        nks[:, 4 * c:4 * c + 4] = np.asarray(r["oks"]).reshape(DEPTH, 4, 128, 4, 64)
        nvs[:, 4 * c:4 * c + 4] = np.asarray(r["ovs"]).reshape(DEPTH, 4, 128, 4, 64)
        nps[:, 4 * c:4 * c + 4] = np.asarray(r["ops"])
        ncs[:, 4 * c:4 * c + 4] = np.asarray(r["ocs"])
    return (y_prompt, y_sample, nkp, nvp, npp, ncp, nks, nvs, nps, ncs)
```
